# Optimizing a Trainium2 kernel written in Bass

```python
import math
import jax, jax.numpy as jnp
from jax import lax
import numpy as np


D_MODEL = 2048
BATCH = 4
SEQ = 8192
DEPTH = 2
DEC_BATCH = 1
DEC_SEQ = 16384
PAST_LEN = 128

N_EVEN = (DEPTH + 1) // 2
N_ODD = DEPTH // 2
EPS = 1e-6
A_WIDTH = D_MODEL // 2
A_HEAD = 64
A_HEADS = A_WIDTH // A_HEAD
A_DECAY_LORA = 64
A_ICL_LORA = 64
A_GATE_LORA = 128
A_COLS = 3 * A_WIDTH + 2 * A_DECAY_LORA + 2 * A_ICL_LORA + A_GATE_LORA
A_SPLITS = (A_WIDTH, 2 * A_WIDTH, 3 * A_WIDTH, 3 * A_WIDTH + 2 * A_DECAY_LORA, 3 * A_WIDTH + 2 * A_DECAY_LORA + 2 * A_ICL_LORA)
RWKV_GN_EPS = 64e-5
DECAY_SCALE = math.exp(-0.5)
B_WIDTH = D_MODEL - A_WIDTH
B_GROUP = 16
B_GROUPS = B_WIDTH // B_GROUP
B_STATE = 64
EVEN_IN = A_COLS + B_WIDTH
C_HEADS = 8
C_NOPE = 64
C_ROPE = 32
C_V = 128
C_Q_RANK = 512
C_KV_RANK = 256
ROPE_THETA = 10000.0
D_HEADS = 8
D_HEAD = 64
D_V = 2 * D_HEAD
SUBLN_EPS = 1e-5
N_BUCKETS = 32
MAX_DISTANCE = 128
ODD_SPLITS = (C_Q_RANK, C_Q_RANK + C_KV_RANK, C_Q_RANK + C_KV_RANK + C_ROPE, C_Q_RANK + C_KV_RANK + C_ROPE + D_HEADS * 2 * D_HEAD, C_Q_RANK + C_KV_RANK + C_ROPE + 2 * D_HEADS * 2 * D_HEAD)
ODD_IN = C_Q_RANK + C_KV_RANK + C_ROPE + 2 * D_HEADS * 2 * D_HEAD + D_HEADS * D_V
Q_BLOCK = 128
FFN_HIDDEN = 5632
N_MOD = 6

kernel_name = 'hybrid_bidir_rwkv7_s5_mla_diffattn_convffn_adaln'


def _f32(t):
    return t.astype(jnp.float32)


def rmsnorm(x, g, eps=EPS):
    xf = _f32(x)
    y = xf * lax.rsqrt(jnp.mean(xf * xf, axis=-1, keepdims=True) + eps)
    return (y * _f32(g)).astype(x.dtype)


def centred_shift(x):
    xp = jnp.pad(x, ((0, 0), (1, 1), (0, 0)))
    return 0.5 * (xp[:, :-2] + xp[:, 2:])


def to_blocks(t):
    b, s = t.shape[0], t.shape[1]
    return jnp.moveaxis(t.reshape((b, s // Q_BLOCK, Q_BLOCK) + t.shape[2:]), 1, 0)


def from_blocks(t):
    t = jnp.moveaxis(t, 0, 1)
    return t.reshape((t.shape[0], t.shape[1] * t.shape[2]) + t.shape[3:])


def rwkv7_scan(r, w, k, v, kk, a, reverse):
    bsz, _, h, n = r.shape
    b = a * kk
    xs = tuple(jnp.moveaxis(t, 1, 0) for t in (r, w, k, v, kk, b))

    def step(state, inp):
        r_t, w_t, k_t, v_t, kk_t, b_t = inp
        sa = jnp.einsum('bhvk,bhk->bhv', state, kk_t)
        state = state * w_t[:, :, None, :] - sa[..., None] * b_t[:, :, None, :] + v_t[..., None] * k_t[:, :, None, :]
        return state, jnp.einsum('bhvk,bhk->bhv', state, r_t)

    s0 = jnp.zeros((bsz, h, n, n), jnp.float32)
    _, ys = lax.scan(step, s0, xs, reverse=reverse)
    return jnp.moveaxis(ys, 0, 1)


def rwkv7_mixer(pa, mu, w0, w_up, a0, a_up, g_up, k_k, k_a, r_k, lnx_g, lnx_b):
    bsz, s, _ = pa.shape
    pa = pa + mu * (centred_shift(pa) - pa)
    r, k, v, dw, da, dg = jnp.split(_f32(pa), A_SPLITS, axis=-1)
    dw = dw.reshape(bsz, s, 2, A_DECAY_LORA)
    da = da.reshape(bsz, s, 2, A_ICL_LORA)
    decay = jnp.exp(-DECAY_SCALE * jax.nn.sigmoid(_f32(w0) + jnp.einsum('bsdr,drc->bsdc', jnp.tanh(dw), _f32(w_up))))
    icl = jax.nn.sigmoid(_f32(a0) + jnp.einsum('bsdr,drc->bsdc', da, _f32(a_up)))
    g = jax.nn.sigmoid(dg) @ _f32(g_up)

    def heads(t):
        return t.reshape(bsz, s, A_HEADS, A_HEAD)

    kk = heads(k * _f32(k_k))
    kk = kk / jnp.maximum(jnp.sqrt(jnp.sum(kk * kk, axis=-1, keepdims=True)), 1e-12)
    r_h, v_h = heads(r), heads(v)
    y = jnp.zeros_like(r_h)
    bonus = jnp.zeros(r_h.shape[:-1] + (1,), jnp.float32)
    for d in range(2):
        k_dh = heads(k * (1.0 + (icl[:, :, d] - 1.0) * _f32(k_a)))
        y = y + rwkv7_scan(r_h, heads(decay[:, :, d]), k_dh, v_h, kk, heads(icl[:, :, d]), reverse=(d == 1))
        bonus = bonus + jnp.sum(r_h * k_dh * _f32(r_k), axis=-1, keepdims=True)
    mean = jnp.mean(y, axis=-1, keepdims=True)
    var = jnp.mean(jnp.square(y - mean), axis=-1, keepdims=True)
    y = ((y - mean) * lax.rsqrt(var + RWKV_GN_EPS)).reshape(bsz, s, A_WIDTH) * _f32(lnx_g) + _f32(lnx_b)
    y = y + (bonus * v_h).reshape(bsz, s, A_WIDTH)
    return (y * g).astype(pa.dtype)


def _complex_combine(e1, e2):
    a1r, a1i, b1r, b1i = e1
    a2r, a2i, b2r, b2i = e2
    return (a2r * a1r - a2i * a1i,
            a2r * a1i + a2i * a1r,
            a2r * b1r - a2i * b1i + b2r,
            a2r * b1i + a2i * b1r + b2i)


def s5_mixer(u, lam_re, lam_im, log_step, b_re, b_im, c_re, c_im, d_skip, w_glu, b_glu):
    bsz, s, _ = u.shape
    uf = _f32(u).reshape(bsz, s, B_GROUPS, B_GROUP)
    y = uf * _f32(d_skip).reshape(B_GROUPS, B_GROUP)
    for d in range(2):
        lr, li = _f32(lam_re[d]), _f32(lam_im[d])
        step = jnp.exp(_f32(log_step[d]))[:, None]
        mag = jnp.exp(lr * step)
        ar, ai = mag * jnp.cos(li * step), mag * jnp.sin(li * step)
        den = lr * lr + li * li
        nr, ni = ar - 1.0, ai
        fr, fi = (nr * lr + ni * li) / den, (ni * lr - nr * li) / den
        br, bi = _f32(b_re[d]), _f32(b_im[d])
        bbr = fr[..., None] * br - fi[..., None] * bi
        bbi = fr[..., None] * bi + fi[..., None] * br
        xr = jnp.einsum('bsgc,gpc->bsgp', uf, bbr)
        xi = jnp.einsum('bsgc,gpc->bsgp', uf, bbi)
        a_r = jnp.broadcast_to(ar, (1, s, B_GROUPS, B_STATE))
        a_i = jnp.broadcast_to(ai, (1, s, B_GROUPS, B_STATE))
        _, _, hr, hi = lax.associative_scan(_complex_combine, (a_r, a_i, xr, xi), reverse=(d == 1), axis=1)
        y = y + jnp.einsum('bsgp,gcp->bsgc', hr, _f32(c_re[d])) - jnp.einsum('bsgp,gcp->bsgc', hi, _f32(c_im[d]))
    z = jax.nn.gelu(y.reshape(bsz, s, B_WIDTH))
    z = z * jax.nn.sigmoid(z @ _f32(w_glu) + _f32(b_glu))
    return z.astype(u.dtype)


def rope_tables(s, dim):
    inv = 1.0 / (ROPE_THETA ** (jnp.arange(0, dim, 2, dtype=jnp.float32) / dim))
    ang = jnp.arange(s, dtype=jnp.float32)[:, None] * inv[None, :]
    return jnp.cos(ang), jnp.sin(ang)


def apply_rope(x, cos, sin):
    x1, x2 = jnp.split(_f32(x), 2, axis=-1)
    return jnp.concatenate([x1 * cos - x2 * sin, x1 * sin + x2 * cos], axis=-1).astype(x.dtype)


def mla_mixer(cq, ckv, kr, q_norm_g, kv_norm_g, w_uq, w_ukv):
    bsz, s, _ = cq.shape
    q = (rmsnorm(cq, q_norm_g) @ w_uq).reshape(bsz, s, C_HEADS, C_NOPE + C_ROPE)
    q_nope, q_rope = q[..., :C_NOPE], q[..., C_NOPE:]
    kv = (rmsnorm(ckv, kv_norm_g) @ w_ukv).reshape(bsz, s, C_HEADS, C_NOPE + C_V)
    k_nope, v = kv[..., :C_NOPE], kv[..., C_NOPE:]
    cos, sin = rope_tables(s, C_ROPE)
    q_rope = apply_rope(q_rope, cos[:, None, :], sin[:, None, :])
    k_rope = apply_rope(kr, cos, sin)
    scale = (C_NOPE + C_ROPE) ** -0.5

    def block(args):
        qn, qr = args
        sc = jnp.einsum('bqhd,bkhd->bhqk', qn, k_nope) + jnp.einsum('bqhd,bkd->bhqk', qr, k_rope)
        p = jax.nn.softmax(_f32(sc) * scale, axis=-1)
        return jnp.einsum('bhqk,bkhd->bqhd', p.astype(v.dtype), v)

    out = from_blocks(lax.map(block, (to_blocks(q_nope), to_blocks(q_rope))))
    return out.reshape(bsz, s, C_HEADS * C_V)


def t5_bucket(rel):
    half = N_BUCKETS // 2
    max_exact = half // 2
    n = jnp.abs(rel)
    large = max_exact + (jnp.log(jnp.maximum(n, 1).astype(jnp.float32) / max_exact) / math.log(MAX_DISTANCE / max_exact) * (half - max_exact)).astype(jnp.int32)
    large = jnp.minimum(large, half - 1)
    return jnp.where(rel > 0, half, 0) + jnp.where(n < max_exact, n, large)


def diff_mixer(q, k, v, rel_bias, lq1, lk1, lq2, lk2, subln_g, lambda_init):
    bsz, s = q.shape[0], q.shape[1]
    lam = jnp.exp(jnp.sum(_f32(lq1) * _f32(lk1))) - jnp.exp(jnp.sum(_f32(lq2) * _f32(lk2))) + lambda_init
    scale = D_HEAD ** -0.5
    kpos = jnp.arange(s, dtype=jnp.int32)

    def block(args):
        qb, blk = args
        qpos = blk * Q_BLOCK + jnp.arange(Q_BLOCK, dtype=jnp.int32)
        bias = jnp.moveaxis(_f32(rel_bias)[t5_bucket(kpos[None, :] - qpos[:, None])], -1, 0)
        sc = _f32(jnp.einsum('bqhmd,bkhmd->bhmqk', qb, k)) * scale + bias[None, :, None]
        p = jax.nn.softmax(sc, axis=-1)
        attn = p[:, :, 0] - lam * p[:, :, 1]
        return jnp.einsum('bhqk,bkhd->bqhd', attn.astype(v.dtype), v)

    nb = s // Q_BLOCK
    out = from_blocks(lax.map(block, (to_blocks(q), jnp.arange(nb, dtype=jnp.int32))))
    out = rmsnorm(out, subln_g, SUBLN_EPS) * (1.0 - lambda_init)
    return out.reshape(bsz, s, D_HEADS * D_V)


def conv_ffn(h, w_up, conv_w, conv_b, w_down):
    u = h @ w_up
    up = jnp.pad(u, ((0, 0), (1, 1), (0, 0)))
    u = up[:, :-2] * conv_w[0] + up[:, 1:-1] * conv_w[1] + up[:, 2:] * conv_w[2] + conv_b
    val, gate = jnp.split(u, 2, axis=-1)
    return (jax.nn.silu(gate) * val) @ w_down


def trunk(x, c, ada_w, ada_b, norm1_g, norm2_g, even_w_in, even_w_out,
          rwkv_mu, rwkv_w0, rwkv_w_up, rwkv_a0, rwkv_a_up, rwkv_g_up, rwkv_k_k, rwkv_k_a, rwkv_r_k, rwkv_lnx_g, rwkv_lnx_b,
          s5_lam_re, s5_lam_im, s5_log_step, s5_b_re, s5_b_im, s5_c_re, s5_c_im, s5_d, s5_w_glu, s5_b_glu,
          odd_w_in, odd_w_out, mla_q_norm_g, mla_kv_norm_g, mla_w_uq, mla_w_ukv,
          diff_lq1, diff_lk1, diff_lq2, diff_lk2, diff_subln_g, rel_bias,
          ffn_w_up, ffn_conv_w, ffn_conv_b, ffn_w_down, final_g):
    cs = jax.nn.silu(c)
    for i in range(DEPTH):
        mod = cs @ ada_w[i] + ada_b[i]
        sh1, sc1, g1, sh2, sc2, g2 = [m[:, None, :] for m in jnp.split(mod, N_MOD, axis=-1)]
        h = rmsnorm(x, norm1_g[i]) * (1.0 + sc1) + sh1
        j = i // 2
        if i % 2 == 0:
            p = h @ even_w_in[j]
            ya = rwkv7_mixer(p[..., :A_COLS], rwkv_mu[j], rwkv_w0[j], rwkv_w_up[j], rwkv_a0[j], rwkv_a_up[j], rwkv_g_up[j],
                             rwkv_k_k[j], rwkv_k_a[j], rwkv_r_k[j], rwkv_lnx_g[j], rwkv_lnx_b[j])
            yb = s5_mixer(p[..., A_COLS:], s5_lam_re[j], s5_lam_im[j], s5_log_step[j], s5_b_re[j], s5_b_im[j],
                          s5_c_re[j], s5_c_im[j], s5_d[j], s5_w_glu[j], s5_b_glu[j])
            mix = jnp.concatenate([ya, yb], axis=-1) @ even_w_out[j]
        else:
            p = h @ odd_w_in[j]
            bsz, s = p.shape[0], p.shape[1]
            cq, ckv, kr, dq, dk, dv = jnp.split(p, ODD_SPLITS, axis=-1)
            yc = mla_mixer(cq, ckv, kr, mla_q_norm_g[j], mla_kv_norm_g[j], mla_w_uq[j], mla_w_ukv[j])
            yd = diff_mixer(dq.reshape(bsz, s, D_HEADS, 2, D_HEAD), dk.reshape(bsz, s, D_HEADS, 2, D_HEAD),
                            dv.reshape(bsz, s, D_HEADS, D_V), rel_bias, diff_lq1[j], diff_lk1[j], diff_lq2[j], diff_lk2[j],
                            diff_subln_g[j], 0.8 - 0.6 * math.exp(-0.3 * i))
            mix = jnp.concatenate([yc, yd], axis=-1) @ odd_w_out[j]
        x = x + g1 * mix
        h = rmsnorm(x, norm2_g[i]) * (1.0 + sc2) + sh2
        x = x + g2 * conv_ffn(h, ffn_w_up[i], ffn_conv_w[i], ffn_conv_b[i], ffn_w_down[i])
    return rmsnorm(x, final_g)


def setup_inputs(seed: int = 0) -> dict:
    key = jax.random.key(seed)
    ks = iter(jax.random.split(key, 64))

    def nrm(shape, s):
        return jax.random.normal(next(ks), shape, jnp.float32) * s

    def uni(shape, lo, hi):
        return jax.random.uniform(next(ks), shape, jnp.float32, lo, hi)

    D = D_MODEL
    inp = {}
    inp['x_prompt'] = nrm((BATCH, SEQ, D), 1.0)
    inp['x_sample'] = nrm((DEC_BATCH, DEC_SEQ, D), 1.0)
    inp['c_prompt'] = nrm((BATCH, D), 1.0)
    inp['c_sample'] = nrm((DEC_BATCH, D), 1.0)
    inp['ada_w'] = nrm((DEPTH, D, N_MOD * D), 0.5 * D ** -0.5)
    inp['ada_b'] = nrm((DEPTH, N_MOD * D), 0.02)
    inp['norm1_g'] = 1.0 + nrm((DEPTH, D), 0.02)
    inp['norm2_g'] = 1.0 + nrm((DEPTH, D), 0.02)
    inp['even_w_in'] = nrm((N_EVEN, D, EVEN_IN), D ** -0.5)
    inp['even_w_out'] = nrm((N_EVEN, A_WIDTH + B_WIDTH, D), (A_WIDTH + B_WIDTH) ** -0.5)
    inp['rwkv_mu'] = uni((N_EVEN, A_COLS), 0.0, 1.0)
    inp['rwkv_w0'] = uni((N_EVEN, 2, A_WIDTH), -6.0, 1.0)
    inp['rwkv_w_up'] = nrm((N_EVEN, 2, A_DECAY_LORA, A_WIDTH), 0.1)
    inp['rwkv_a0'] = nrm((N_EVEN, 2, A_WIDTH), 0.1)
    inp['rwkv_a_up'] = nrm((N_EVEN, 2, A_ICL_LORA, A_WIDTH), 0.5 * A_ICL_LORA ** -0.5)
    inp['rwkv_g_up'] = nrm((N_EVEN, A_GATE_LORA, A_WIDTH), A_GATE_LORA ** -0.5)
    inp['rwkv_k_k'] = 0.85 + nrm((N_EVEN, A_WIDTH), 0.05)
    inp['rwkv_k_a'] = 1.0 + nrm((N_EVEN, A_WIDTH), 0.05)
    inp['rwkv_r_k'] = nrm((N_EVEN, A_HEADS, A_HEAD), 0.1)
    inp['rwkv_lnx_g'] = 1.0 + nrm((N_EVEN, A_WIDTH), 0.02)
    inp['rwkv_lnx_b'] = nrm((N_EVEN, A_WIDTH), 0.02)
    inp['s5_lam_re'] = -0.5 + nrm((N_EVEN, 2, B_GROUPS, B_STATE), 0.01)
    inp['s5_lam_im'] = jnp.pi * jnp.arange(B_STATE, dtype=jnp.float32) + nrm((N_EVEN, 2, B_GROUPS, B_STATE), 0.01)
    inp['s5_log_step'] = uni((N_EVEN, 2, B_GROUPS), math.log(1e-3), math.log(1e-1))
    inp['s5_b_re'] = nrm((N_EVEN, 2, B_GROUPS, B_STATE, B_GROUP), (2 * B_GROUP) ** -0.5)
    inp['s5_b_im'] = nrm((N_EVEN, 2, B_GROUPS, B_STATE, B_GROUP), (2 * B_GROUP) ** -0.5)
    inp['s5_c_re'] = nrm((N_EVEN, 2, B_GROUPS, B_GROUP, B_STATE), B_STATE ** -0.5)
    inp['s5_c_im'] = nrm((N_EVEN, 2, B_GROUPS, B_GROUP, B_STATE), B_STATE ** -0.5)
    inp['s5_d'] = nrm((N_EVEN, B_WIDTH), 1.0)
    inp['s5_w_glu'] = nrm((N_EVEN, B_WIDTH, B_WIDTH), B_WIDTH ** -0.5)
    inp['s5_b_glu'] = nrm((N_EVEN, B_WIDTH), 0.02)
    inp['odd_w_in'] = nrm((N_ODD, D, ODD_IN), D ** -0.5)
    inp['odd_w_out'] = nrm((N_ODD, C_HEADS * C_V + D_HEADS * D_V, D), (C_HEADS * C_V + D_HEADS * D_V) ** -0.5)
    inp['mla_q_norm_g'] = 1.0 + nrm((N_ODD, C_Q_RANK), 0.02)
    inp['mla_kv_norm_g'] = 1.0 + nrm((N_ODD, C_KV_RANK), 0.02)
    inp['mla_w_uq'] = nrm((N_ODD, C_Q_RANK, C_HEADS * (C_NOPE + C_ROPE)), C_Q_RANK ** -0.5)
    inp['mla_w_ukv'] = nrm((N_ODD, C_KV_RANK, C_HEADS * (C_NOPE + C_V)), C_KV_RANK ** -0.5)
    inp['diff_lq1'] = nrm((N_ODD, D_HEAD), 0.1)
    inp['diff_lk1'] = nrm((N_ODD, D_HEAD), 0.1)
    inp['diff_lq2'] = nrm((N_ODD, D_HEAD), 0.1)
    inp['diff_lk2'] = nrm((N_ODD, D_HEAD), 0.1)
    inp['diff_subln_g'] = 1.0 + nrm((N_ODD, D_V), 0.02)
    inp['rel_bias'] = nrm((N_BUCKETS, D_HEADS), 0.5)
    inp['ffn_w_up'] = nrm((DEPTH, D, 2 * FFN_HIDDEN), D ** -0.5)
    inp['ffn_conv_w'] = jnp.array([0.25, 0.5, 0.25], jnp.float32)[None, :, None] + nrm((DEPTH, 3, 2 * FFN_HIDDEN), 0.1)
    inp['ffn_conv_b'] = nrm((DEPTH, 2 * FFN_HIDDEN), 0.02)
    inp['ffn_w_down'] = nrm((DEPTH, FFN_HIDDEN, D), FFN_HIDDEN ** -0.5)
    inp['final_g'] = 1.0 + nrm((D,), 0.02)
    return inp


def reference(x_prompt, x_sample, c_prompt, c_sample, ada_w, ada_b, norm1_g, norm2_g, even_w_in, even_w_out,
              rwkv_mu, rwkv_w0, rwkv_w_up, rwkv_a0, rwkv_a_up, rwkv_g_up, rwkv_k_k, rwkv_k_a, rwkv_r_k, rwkv_lnx_g, rwkv_lnx_b,
              s5_lam_re, s5_lam_im, s5_log_step, s5_b_re, s5_b_im, s5_c_re, s5_c_im, s5_d, s5_w_glu, s5_b_glu,
              odd_w_in, odd_w_out, mla_q_norm_g, mla_kv_norm_g, mla_w_uq, mla_w_ukv,
              diff_lq1, diff_lk1, diff_lq2, diff_lk2, diff_subln_g, rel_bias,
              ffn_w_up, ffn_conv_w, ffn_conv_b, ffn_w_down, final_g):
    weights = (ada_w, ada_b, norm1_g, norm2_g, even_w_in, even_w_out,
               rwkv_mu, rwkv_w0, rwkv_w_up, rwkv_a0, rwkv_a_up, rwkv_g_up, rwkv_k_k, rwkv_k_a, rwkv_r_k, rwkv_lnx_g, rwkv_lnx_b,
               s5_lam_re, s5_lam_im, s5_log_step, s5_b_re, s5_b_im, s5_c_re, s5_c_im, s5_d, s5_w_glu, s5_b_glu,
               odd_w_in, odd_w_out, mla_q_norm_g, mla_kv_norm_g, mla_w_uq, mla_w_ukv,
               diff_lq1, diff_lk1, diff_lq2, diff_lk2, diff_subln_g, rel_bias,
               ffn_w_up, ffn_conv_w, ffn_conv_b, ffn_w_down, final_g)
    y_prompt = trunk(x_prompt, c_prompt, *weights)
    y_sample = trunk(x_sample, c_sample, *weights)
    return (y_prompt, y_sample)
```

```python
import contextlib
import numpy as np
import concourse.bass as bass
import concourse.mybir as mybir

F32 = mybir.dt.float32
BF16 = mybir.dt.bfloat16
I32 = mybir.dt.int32
ALU = mybir.AluOpType
AF = mybir.ActivationFunctionType
AX = mybir.AxisListType


class Prog:
    ENGS = ("pe", "dve", "act", "pool", "sp")

    def __init__(self, nc, es):
        self.nc = nc
        self.es = es
        self.eng = {"pe": nc.tensor, "dve": nc.vector, "act": nc.scalar,
                    "pool": nc.gpsimd, "sp": nc.sync}
        self.NDS = 6
        self.semnames = ["pe", "dve", "act", "pool"] + [f"d{q}{i}" for q in ("sp", "pool", "act") for i in range(self.NDS)]
        self.dmai = {"sp": 0, "pool": 0, "act": 0}
        self.sem = {n: es.enter_context(nc.semaphore("s_" + n)) for n in self.semnames}
        self.cnt = {n: 0 for n in self.semnames}
        self.waited = {e: {n: 0 for n in self.semnames} for e in self.ENGS}
        self.lastw = {}
        self.reads = {}
        self.nins = 0
        self._ps = None
        self._psi = 0
        self._uid = 0

    def sb(self, name, shape, dtype=F32, es=None):
        es = es or self.es
        self._uid += 1
        nm = f"{name}_{self._uid}"
        t = es.enter_context(self.nc.sbuf_tensor(nm, list(shape), dtype))
        return t

    def psum_pool(self, n=8):
        self._ps = []
        for i in range(n):
            t = self.es.enter_context(self.nc.psum_tensor(f"ps{i}", [128, 512], F32))
            self._ps.append(t)

    def ps_at(self, i):
        return self._ps[i]

    def ps(self):
        t = self._ps[self._psi % len(self._ps)]
        self._psi += 1
        return t

    def op(self, eng, fn, r=(), w=(), dma=False, nosame=False):
        if dma:
            semname = f"d{eng}{self.dmai[eng] % self.NDS}"
            self.dmai[eng] += 1
        else:
            semname = eng
        deps = {}
        if dma and self.cnt[semname] > 0:
            deps[semname] = self.cnt[semname]
        for k in list(r) + list(w):
            for s_, c_ in self.lastw.get(k, {}).items():
                if deps.get(s_, 0) < c_:
                    deps[s_] = c_
        for k in w:
            for s, c in self.reads.get(k, {}).items():
                if deps.get(s, 0) < c:
                    deps[s] = c
        e = self.eng[eng]
        for s, c in deps.items():
            if (not dma) and s == eng and (eng == "pe" or nosame):
                continue
            if self.waited[eng][s] >= c:
                continue
            e.wait_ge(self.sem[s], c)
            self.waited[eng][s] = c
            self.nins += 1
        ins = fn(e)
        inc = 16 if dma else 1
        ins.then_inc(self.sem[semname], inc)
        self.cnt[semname] += inc
        c = self.cnt[semname]
        self.nins += 1
        for k in w:
            self.lastw.setdefault(k, {})[semname] = c
            self.reads[k] = {}
        for k in r:
            d = self.reads.setdefault(k, {})
            d[semname] = c
        return ins

    def dma(self, eng, out, in_, r=(), w=(), **kw):
        return self.op(eng, lambda e: e.dma_start(out=out, in_=in_, **kw), r=r, w=w, dma=True)

    def barrier(self):
        for en in self.ENGS:
            e = self.eng[en]
            for s in self.semnames:
                c = self.cnt[s]
                if c > self.waited[en][s] and not (s == en and en == "pe"):
                    e.wait_ge(self.sem[s], c)
                    self.waited[en][s] = c
                    self.nins += 1

    def finish(self):
        e = self.eng["sp"]
        for s in self.semnames:
            if self.cnt[s] > 0:
                e.wait_ge(self.sem[s], self.cnt[s])


def K(*ts):
    return [t if isinstance(t, str) else t.name for t in ts]


import contextlib
import numpy as np

D = 2048
KC = 16
EPS = 1e-6


class Cfg:
    def __init__(self, seqs, TT=512):
        self.seqs = list(seqs)
        self.NS = len(seqs)
        self.off = [int(v) for v in np.cumsum([0] + self.seqs[:-1])]
        self.NTOK = int(sum(seqs))
        self.TT = TT


def lay_vec(v, nch=None):
    v = np.asarray(v, np.float32).reshape(-1, 128)
    return np.ascontiguousarray(v.T)


def compute_mod(pg, es, cT_d, adaw_d, adab_d, ncc, NS, mod_tile=None):
    cT = pg.sb("cT", [128, KC, NS], F32, es)
    pg.dma('sp', cT[:], cT_d, w=K(cT))
    csT = pg.sb("csT", [128, KC, NS], F32, es)
    pg.op('act', lambda e: e.activation(csT[:], cT[:], AF.Silu), r=K(cT), w=K(csT))
    bt = pg.sb("adab", [128, ncc], F32, es)
    pg.dma('sp', bt[:], adab_d, w=K(bt))
    mod = mod_tile if mod_tile is not None else pg.sb("mod", [128, ncc, NS], F32, es)
    wblk = [pg.sb("wblk", [128, KC, 128], F32, es) for _ in range(2)]
    for j in range(ncc):
        wb = wblk[j % 2]
        pg.dma('sp', wb[:], adaw_d[:, j * 128:(j + 1) * 128].rearrange("(kc p) f -> p kc f", p=128), w=K(wb))
        ps = pg.ps()
        for kc in range(KC):
            pg.op('pe', lambda e: e.matmul(ps[:, 0:NS], lhsT=wb[:, kc, :], rhs=csT[:, kc, :], start=(kc == 0), stop=(kc == KC - 1)),
                  r=K(wb, csT), w=K(ps))
        pg.op('dve', lambda e: e.tensor_scalar(mod[:, j, :], ps[:, 0:NS], bt[:, j:j + 1], None, ALU.add), r=K(ps, bt), w=K(mod))
    return mod


def load_w_bf16(pg, es, w_d, ncols, name):
    wbf = pg.sb(name, [128, KC, ncols], BF16, es)
    stg = [pg.sb("wstg", [128, ncols], F32, es) for _ in range(2)]
    for kc in range(KC):
        s = stg[kc % 2]
        pg.dma('sp', s[:], w_d[kc * 128:(kc + 1) * 128, :], w=K(s))
        pg.op('dve', lambda e: e.tensor_copy(wbf[:, kc, :], s[:]), r=K(s), w=K(wbf))
    return wbf


def norm_mod_T(pg, cfg, es, x_d, t0, ntok, b, ident, A_s, sh, hT, bufs, cnt):
    xb, xsb, ssb, jk = bufs
    for j in range(ntok // 128):
        n = cnt[0]; cnt[0] += 1
        xt = xb[n % len(xb)]; xs = xsb[n % len(xsb)]; ss = ssb[n % len(ssb)]
        pg.dma('sp', xt[:], x_d[t0 + j * 128:t0 + (j + 1) * 128, :], w=K(xt))
        pg.op('act', lambda e: e.activation(jk[:], xt[:], AF.Square, accum_out=ss[:, 0:1]), r=K(xt), w=K(jk, ss))
        pg.op('act', lambda e: e.activation(ss[:, 1:2], ss[:, 0:1], AF.Sqrt, bias=EPS, scale=1.0 / D), r=K(ss), w=K(ss))
        pg.op('dve', lambda e: e.reciprocal(ss[:, 2:3], ss[:, 1:2]), r=K(ss), w=K(ss))
        pg.op('pool', lambda e: e.tensor_scalar(xs[:], xt[:], ss[:, 2:3], None, ALU.mult), r=K(xt, ss), w=K(xs))
        for g in range(4):
            ps = pg.ps()
            for i in range(4):
                kc = g * 4 + i
                pg.op('pe', lambda e: e.transpose(ps[:, i * 128:(i + 1) * 128], xs[:, kc * 128:(kc + 1) * 128], ident[:]),
                      r=K(xs, ident), w=K(ps))
            for i in range(4):
                kc = g * 4 + i
                eng = 'dve' if i % 2 == 0 else 'act'
                if eng == 'dve':
                    pg.op('dve', lambda e: e.tensor_scalar(hT[:, kc, j * 128:(j + 1) * 128], ps[:, i * 128:(i + 1) * 128],
                                                           A_s[:, kc, b:b + 1], sh[:, kc, b:b + 1], ALU.mult, ALU.add),
                          r=K(ps, A_s, sh), w=K(hT))
                else:
                    pg.op('act', lambda e: e.activation(hT[:, kc, j * 128:(j + 1) * 128], ps[:, i * 128:(i + 1) * 128], AF.Identity,
                                                        bias=sh[:, kc, b:b + 1], scale=A_s[:, kc, b:b + 1]),
                          r=K(ps, A_s, sh), w=K(hT))


def norm_bufs(pg, es):
    xb = [pg.sb("xt", [128, D], F32, es) for _ in range(2)]
    xsb = [pg.sb("xs", [128, D], F32, es) for _ in range(1)]
    ssb = [pg.sb("ss", [128, 4], F32, es) for _ in range(4)]
    jk = pg.sb("jk", [128, D], F32, es)
    return (xb, xsb, ssb, jk)


def phase_A1(pg, cfg, d, ncc=7):
    NS, TT = cfg.NS, cfg.TT
    NCOL = ncc * 128
    with contextlib.ExitStack() as es:
        ident = pg.sb("ident", [128, 128], F32, es)
        pg.dma('sp', ident[:], d['ident'], w=K(ident))
        mod = compute_mod(pg, es, d['cT'], d['adaw_A'], d['adab_A'], 32, NS)
        g1t = pg.sb("g1t", [128, KC], F32, es)
        pg.dma('sp', g1t[:], d['norm1_g0'], w=K(g1t))
        A_s = pg.sb("A_s", [128, KC, NS], F32, es)
        sh = pg.sb("sh", [128, KC, NS], F32, es)
        for b in range(NS):
            pg.op('dve', lambda e: e.scalar_tensor_tensor(A_s[:, :, b], mod[:, 16:32, b], 1.0, g1t[:, :], ALU.add, ALU.mult),
                  r=K(mod, g1t), w=K(A_s))
        pg.op('dve', lambda e: e.tensor_copy(sh[:], mod[:, 0:16, :]), r=K(mod), w=K(sh))
        wbf = load_w_bf16(pg, es, d['w_inA'], NCOL, "winA")
        if 'dbg_mod' in d:
            pg.dma('sp', d['dbg_mod'], mod[:], r=K(mod), w=['dbg_mod'])
            pg.dma('sp', d['dbg_As'], A_s[:], r=K(A_s), w=['dbg_As'])
        bufs = norm_bufs(pg, es)
        hTb = [pg.sb("hT", [128, KC, TT], BF16, es) for _ in range(2)]
        pob = [pg.sb("po", [128, ncc, TT], F32, es) for _ in range(2)]
        cnt = [0]
        it = 0
        PT = d['PT'].rearrange("(c p) t -> p c t", p=128)
        for b in range(NS):
            for ti in range(cfg.seqs[b] // TT):
                t0 = cfg.off[b] + ti * TT
                hT = hTb[it % 2]; po = pob[it % 2]; it += 1
                norm_mod_T(pg, cfg, es, d['x_all'], t0, TT, b, ident, A_s, sh, hT, bufs, cnt)
                if 'dbg_hT' in d and it == 1:
                    hf = pg.sb('hf', [128, KC, TT], F32, es)
                    pg.op('dve', lambda e: e.tensor_copy(hf[:], hT[:]), r=K(hT), w=K(hf))
                    pg.dma('sp', d['dbg_hT'], hf[:], r=K(hf), w=['dbg_hT'])
                for c in range(ncc):
                    ps = pg.ps()
                    for kc in range(KC):
                        pg.op('pe', lambda e: e.matmul(ps[:, 0:TT], lhsT=wbf[:, kc, c * 128:(c + 1) * 128], rhs=hT[:, kc, :],
                                                       start=(kc == 0), stop=(kc == KC - 1)), r=K(wbf, hT), w=K(ps))
                    eng = 'act' if c % 2 == 0 else 'dve'
                    if eng == 'act':
                        pg.op('act', lambda e: e.copy(po[:, c, :], ps[:, 0:TT]), r=K(ps), w=K(po))
                    else:
                        pg.op('dve', lambda e: e.tensor_copy(po[:, c, :], ps[:, 0:TT]), r=K(ps), w=K(po))
                pg.dma('pool', PT[:, :, t0:t0 + TT], po[:], r=K(po), w=["PT"])


DECAY_SCALE = float(np.exp(-0.5))
GN_EPS = 64e-5


def phase_A2(pg, cfg, d):
    pg.barrier()
    NS, TT = cfg.NS, cfg.TT
    with contextlib.ExitStack() as es:
        ident = pg.sb("ident", [128, 128], F32, es)
        pg.dma('sp', ident[:], d['ident'], w=K(ident))
        bones = pg.sb("bones", [128, 128], F32, es)
        pg.dma('sp', bones[:], d['bones'], w=K(bones))
        sv = pg.sb("svec", [128, 24], F32, es)
        pg.dma('sp', sv[:], d['rwkv_sv'], w=K(sv))
        hmu = pg.sb("hmu", [128, 6], F32, es); omu = pg.sb("omu", [128, 6], F32, es)
        pg.op('dve', lambda e: e.tensor_scalar(hmu[:], sv[:, 0:6], 0.5, None, ALU.mult), r=K(sv), w=K(hmu))
        pg.op('dve', lambda e: e.tensor_scalar(omu[:], sv[:, 0:6], -1.0, 1.0, ALU.mult, ALU.add), r=K(sv), w=K(omu))
        wup = pg.sb("wup", [128, 128], F32, es); aup = pg.sb("aup", [128, 128], F32, es); gup = pg.sb("gup", [128, 128], F32, es)
        pg.dma('sp', wup[:], d['wup'], w=K(wup)); pg.dma('sp', aup[:], d['aup'], w=K(aup)); pg.dma('sp', gup[:], d['gup'], w=K(gup))
        raw = [[pg.sb("raw", [128, TT + 2], F32, es) for _ in range(6)] for _ in range(2)]
        m = [pg.sb("m", [128, TT], F32, es) for _ in range(6)]
        t1 = pg.sb("t1", [128, TT], F32, es)
        tdw = pg.sb("tdw", [128, TT], F32, es)
        sg = pg.sb("sg", [128, TT], F32, es)
        icl = pg.sb("icl", [128, TT], F32, es)
        kk = pg.sb("kk", [128, TT], F32, es); kk2 = pg.sb("kk2", [128, TT], F32, es)
        kinds = [pg.sb("kind", [128, TT], F32, es) for _ in range(8)]
        rk = pg.sb("rk", [128, TT], F32, es)
        gt = pg.sb("gt", [128, TT], F32, es); bv = pg.sb("bv", [128, TT], F32, es)
        tm = [pg.sb("tm", [128, 8, 128], F32, es) for _ in range(2)]
        PT = d['PT']
        VT = d['VT']
        it = 0
        for b in range(NS):
            S = cfg.seqs[b]
            for ti in range(S // TT):
                t0 = cfg.off[b] + ti * TT
                rw = raw[it % 2]; it += 1
                lo = 1 if ti == 0 else 0
                hi = 1 if ti == S // TT - 1 else 0
                for c in range(6):
                    if lo:
                        pg.op('pool', lambda e: e.memset(rw[c][:, 0:1], 0.0), w=K(rw[c]))
                    if hi:
                        pg.op('pool', lambda e: e.memset(rw[c][:, TT + 1:TT + 2], 0.0), w=K(rw[c]))
                    pg.dma('sp', rw[c][:, lo:TT + 2 - hi], PT[c * 128:(c + 1) * 128, t0 - 1 + lo:t0 + TT + 1 - hi], r=["PT"], w=K(rw[c]))
                    pg.op('dve', lambda e: e.tensor_tensor(t1[:], rw[c][:, 0:TT], rw[c][:, 2:TT + 2], ALU.add), r=K(rw[c]), w=K(t1))
                    pg.op('dve', lambda e: e.tensor_scalar(t1[:], t1[:], hmu[:, c:c + 1], None, ALU.mult), r=K(t1, hmu), w=K(t1))
                    mc = m[c] if c != 0 else kinds[1]
                    pg.op('dve', lambda e: e.scalar_tensor_tensor(mc[:], rw[c][:, 1:TT + 1], omu[:, c:c + 1], t1[:], ALU.mult, ALU.add),
                          r=K(rw[c], omu, t1), w=K(mc))
                m_r = kinds[1]; m_k = m[1]; m_v = m[2]; m_dw = m[3]; m_da = m[4]; m_dg = m[5]
                pg.op('act', lambda e: e.activation(tdw[:], m_dw[:], AF.Tanh), r=K(m_dw), w=K(tdw))
                pg.op('dve', lambda e: e.tensor_scalar(kk[:], m_k[:], sv[:, 10:11], None, ALU.mult), r=K(m_k, sv), w=K(kk))
                pg.op('dve', lambda e: e.tensor_tensor(kk2[:], kk[:], kk[:], ALU.mult), r=K(kk), w=K(kk2))
                ps = pg.ps()
                pg.op('pe', lambda e: e.matmul(ps[:, 0:TT], lhsT=bones[:], rhs=kk2[:], start=True, stop=True), r=K(bones, kk2), w=K(ps))
                pg.op('act', lambda e: e.activation(kk2[:], ps[:, 0:TT], AF.Sqrt), r=K(ps), w=K(kk2))
                pg.op('dve', lambda e: e.tensor_scalar(kk2[:], kk2[:], 1e-12, None, ALU.max), r=K(kk2), w=K(kk2))
                pg.op('dve', lambda e: e.reciprocal(kk2[:], kk2[:]), r=K(kk2), w=K(kk2))
                pg.op('dve', lambda e: e.tensor_tensor(kinds[0][:], kk[:], kk2[:], ALU.mult), r=K(kk, kk2), w=K(kinds[0]))
                psb = pg.ps()
                for dd in range(2):
                    kw, kb, kkd = kinds[2 + 3 * dd], kinds[3 + 3 * dd], kinds[4 + 3 * dd]
                    ps = pg.ps()
                    pg.op('pe', lambda e: e.matmul(ps[:, 0:TT], lhsT=wup[64 * dd:64 * dd + 64, :], rhs=tdw[64 * dd:64 * dd + 64, :], start=True, stop=True),
                          r=K(wup, tdw), w=K(ps))
                    pg.op('act', lambda e: e.activation(sg[:], ps[:, 0:TT], AF.Sigmoid, bias=sv[:, 6 + dd:7 + dd]), r=K(ps, sv), w=K(sg))
                    pg.op('act', lambda e: e.activation(kw[:], sg[:], AF.Exp, scale=-DECAY_SCALE), r=K(sg), w=K(kw))
                    ps2 = pg.ps()
                    pg.op('pe', lambda e: e.matmul(ps2[:, 0:TT], lhsT=aup[64 * dd:64 * dd + 64, :], rhs=m_da[64 * dd:64 * dd + 64, :], start=True, stop=True),
                          r=K(aup, m_da), w=K(ps2))
                    pg.op('act', lambda e: e.activation(icl[:], ps2[:, 0:TT], AF.Sigmoid, bias=sv[:, 8 + dd:9 + dd]), r=K(ps2, sv), w=K(icl))
                    pg.op('dve', lambda e: e.tensor_tensor(kb[:], icl[:], kinds[0][:], ALU.mult), r=K(icl, kinds[0]), w=K(kb))
                    pg.op('dve', lambda e: e.tensor_scalar(t1[:], icl[:], -1.0, sv[:, 11:12], ALU.add, ALU.mult), r=K(icl, sv), w=K(t1))
                    pg.op('dve', lambda e: e.scalar_tensor_tensor(kkd[:], t1[:], 1.0, m_k[:], ALU.add, ALU.mult), r=K(t1, m_k), w=K(kkd))
                    pg.op('dve', lambda e: e.scalar_tensor_tensor(rk[:], m_r[:], sv[:, 12:13], kkd[:], ALU.mult, ALU.mult), r=K(m_r, sv, kkd), w=K(rk))
                    pg.op('pe', lambda e: e.matmul(psb[:, 0:TT], lhsT=bones[:], rhs=rk[:], start=(dd == 0), stop=(dd == 1)), r=K(bones, rk), w=K(psb))
                pg.op('dve', lambda e: e.tensor_tensor(bv[:], psb[:, 0:TT], m_v[:], ALU.mult), r=K(psb, m_v), w=K(bv))
                pg.op('act', lambda e: e.activation(sg[:], m_dg[:], AF.Sigmoid), r=K(m_dg), w=K(sg))
                ps = pg.ps()
                pg.op('pe', lambda e: e.matmul(ps[:, 0:TT], lhsT=gup[:], rhs=sg[:], start=True, stop=True), r=K(gup, sg), w=K(ps))
                pg.op('act', lambda e: e.copy(gt[:], ps[:, 0:TT]), r=K(ps), w=K(gt))
                pg.dma('pool', d['MV'][:, t0:t0 + TT], m_v[:], r=K(m_v), w=["MV"])
                pg.dma('pool', d['BV'][:, t0:t0 + TT], bv[:], r=K(bv), w=["BV"])
                pg.dma('pool', d['GG'][:, t0:t0 + TT], gt[:], r=K(gt), w=["GG"])
                for j in range(TT // 128):
                    tmt = tm[j % 2]
                    for g2 in range(2):
                        ps = pg.ps()
                        for i in range(4):
                            kd = g2 * 4 + i
                            pg.op('pe', lambda e: e.transpose(ps[:, i * 128:(i + 1) * 128], kinds[kd][:, j * 128:(j + 1) * 128], ident[:]),
                                  r=K(kinds[kd], ident), w=K(ps))
                        if g2 == 0:
                            pg.op('act', lambda e: e.copy(tmt[:, 0:4, :], ps[:, :].rearrange("p (a b) -> p a b", a=4)), r=K(ps), w=K(tmt))
                        else:
                            pg.op('dve', lambda e: e.tensor_copy(tmt[:, 4:8, :], ps[:, :].rearrange("p (a b) -> p a b", a=4)), r=K(ps), w=K(tmt))
                    tt0 = t0 + j * 128
                    pg.dma('pool', VT[0:5, :, tt0:tt0 + 128, :].rearrange("a h t k -> t a h k"),
                           tmt[:, 0:5, :].rearrange("t a (h k) -> t a h k", h=2), r=K(tmt), w=["VT"])
                    pg.dma('pool', VT[5:7, :, tt0:tt0 + 128, :].rearrange("a h t k -> t a h k"),
                           tmt[:, 0:2, :].rearrange("t a (h k) -> t a h k", h=2), r=K(tmt), w=["VT"])
                    pg.dma('pool', VT[7:10, :, tt0:tt0 + 128, :].rearrange("a h t k -> t a h k"),
                           tmt[:, 5:8, :].rearrange("t a (h k) -> t a h k", h=2), r=K(tmt), w=["VT"])


def bcast_vt(VT, NTOK, kind0, nk, h, tok0, TB):
    off = VT.offset + (kind0 * 2 + h) * NTOK * 64 + tok0 * 64
    return bass.AP(VT.tensor, off, [[0, 64], [2 * NTOK * 64, nk], [64, TB], [1, 64]])


def phase_A3(pg, cfg, d, TB=4, VB=256, dir_eng=('dve', 'dve')):
    pg.barrier()
    NS = cfg.NS
    NTOK = cfg.NTOK
    seqs = cfg.seqs
    Smax = max(seqs)
    VT = d['VT']
    with contextlib.ExitStack() as es:
        for dd in range(2):
            E = dir_eng[dd]
            St = pg.sb("St", [128, NS, 64], F32, es)
            tmp = pg.sb("tmp", [128, NS, 64], F32, es)
            sa = pg.sb("sa", [128, NS], F32, es)
            G = [pg.sb("G", [128, NS, 5, TB, 64], F32, es) for _ in range(2)]
            Vb = [pg.sb("Vb", [128, NS, VB], F32, es) for _ in range(2)]
            Yb = [pg.sb("Yb", [128, NS, VB], F32, es) for _ in range(2)]
            YD = d['YF'] if dd == 0 else d['YB']
            pg.op(E, lambda e: e.memset(St[:], 0.0), w=K(St))
            for i in range(Smax):
                act = [b for b in range(NS) if seqs[b] > i]
                b_lo = act[0]
                nb = NS - b_lo
                assert act == list(range(b_lo, NS))
                if i % VB == 0:
                    vb = Vb[(i // VB) % 2]; yb = Yb[(i // VB) % 2]
                    for b in act:
                        tok0 = cfg.off[b] + (i if dd == 0 else seqs[b] - i - VB)
                        pg.dma('sp', vb[:, b, :], d['MV'][:, tok0:tok0 + VB], r=["MV"], w=K(vb))
                if i % TB == 0:
                    g = G[(i // TB) % 2]
                    for b in act:
                        tok0 = cfg.off[b] + (i if dd == 0 else seqs[b] - i - TB)
                        for h in range(2):
                            pg.dma('sp', g[64 * h:64 * h + 64, b, :, :, :],
                                   bcast_vt(VT, NTOK, 5 * dd, 5, h, tok0, TB), r=["VT"], w=K(g))
                tb = (i % TB) if dd == 0 else TB - 1 - (i % TB)
                vi = (i % VB) if dd == 0 else VB - 1 - (i % VB)
                Gkk = g[:, b_lo:, 0, tb, :]; Gr = g[:, b_lo:, 1, tb, :]; Gw = g[:, b_lo:, 2, tb, :]; Gb = g[:, b_lo:, 3, tb, :]; Gk = g[:, b_lo:, 4, tb, :]
                S_ = St[:, b_lo:, :]; T_ = tmp[:, b_lo:, :]
                pg.op(E, lambda e: e.tensor_tensor(T_, S_, Gkk, ALU.mult), r=K(St, g), w=K(tmp))
                pg.op(E, lambda e: e.tensor_reduce(sa[:, b_lo:], T_, AX.X, ALU.add), r=K(tmp), w=K(sa))
                pg.op(E, lambda e: e.tensor_tensor(S_, S_, Gw, ALU.mult), r=K(St, g), w=K(St))
                pg.op(E, lambda e: e.tensor_tensor(T_, Gb, sa[:, b_lo:].unsqueeze(2).to_broadcast([128, nb, 64]), ALU.mult), r=K(sa, g), w=K(tmp))
                pg.op(E, lambda e: e.tensor_tensor(S_, S_, T_, ALU.subtract), r=K(St, tmp), w=K(St))
                pg.op(E, lambda e: e.tensor_tensor(T_, Gk, vb[:, b_lo:, vi:vi + 1].to_broadcast([128, nb, 64]), ALU.mult), r=K(vb, g), w=K(tmp))
                pg.op(E, lambda e: e.tensor_tensor(S_, S_, T_, ALU.add), r=K(St, tmp), w=K(St))
                pg.op(E, lambda e: e.tensor_tensor(T_, S_, Gr, ALU.mult), r=K(St, g), w=K(tmp))
                pg.op(E, lambda e: e.tensor_reduce(yb[:, b_lo:, vi], T_, AX.X, ALU.add), r=K(tmp), w=K(yb))
                if i % VB == VB - 1:
                    for b in act:
                        tok0 = cfg.off[b] + (i - VB + 1 if dd == 0 else seqs[b] - i - 1)
                        pg.dma('pool', YD[:, tok0:tok0 + VB], yb[:, b, :], r=K(yb), w=["YF" if dd == 0 else "YB"])


def phase_A4(pg, cfg, d):
    pg.barrier()
    NS, TT = cfg.NS, cfg.TT
    with contextlib.ExitStack() as es:
        bones = pg.sb("bones", [128, 128], F32, es)
        pg.dma('sp', bones[:], d['bones'], w=K(bones))
        sv = pg.sb("svec", [128, 24], F32, es)
        pg.dma('sp', sv[:], d['rwkv_sv'], w=K(sv))
        bufs = [[pg.sb("a4", [128, TT], F32, es) for _ in range(4)] for _ in range(2)]
        y = pg.sb("y", [128, TT], F32, es); yc = pg.sb("yc", [128, TT], F32, es); y2 = pg.sb("y2", [128, TT], F32, es)
        rs = pg.sb("rs", [128, TT], F32, es)
        for it in range(cfg.NTOK // TT):
            t0 = it * TT
            yf, ybk, bv, gg = bufs[it % 2]
            pg.dma('sp', yf[:], d['YF'][:, t0:t0 + TT], r=["YF"], w=K(yf))
            pg.dma('sp', ybk[:], d['YB'][:, t0:t0 + TT], r=["YB"], w=K(ybk))
            pg.dma('sp', bv[:], d['BV'][:, t0:t0 + TT], r=["BV"], w=K(bv))
            pg.dma('sp', gg[:], d['GG'][:, t0:t0 + TT], r=["GG"], w=K(gg))
            pg.op('dve', lambda e: e.tensor_tensor(y[:], yf[:], ybk[:], ALU.add), r=K(yf, ybk), w=K(y))
            ps = pg.ps()
            pg.op('pe', lambda e: e.matmul(ps[:, 0:TT], lhsT=bones[:], rhs=y[:], start=True, stop=True), r=K(bones, y), w=K(ps))
            pg.op('dve', lambda e: e.scalar_tensor_tensor(yc[:], ps[:, 0:TT], -1.0 / 64, y[:], ALU.mult, ALU.add), r=K(ps, y), w=K(yc))
            pg.op('dve', lambda e: e.tensor_tensor(y2[:], yc[:], yc[:], ALU.mult), r=K(yc), w=K(y2))
            ps2 = pg.ps()
            pg.op('pe', lambda e: e.matmul(ps2[:, 0:TT], lhsT=bones[:], rhs=y2[:], start=True, stop=True), r=K(bones, y2), w=K(ps2))
            pg.op('act', lambda e: e.activation(rs[:], ps2[:, 0:TT], AF.Sqrt, bias=GN_EPS, scale=1.0 / 64), r=K(ps2), w=K(rs))
            pg.op('dve', lambda e: e.reciprocal(rs[:], rs[:]), r=K(rs), w=K(rs))
            pg.op('dve', lambda e: e.tensor_tensor(yc[:], yc[:], rs[:], ALU.mult), r=K(yc, rs), w=K(yc))
            pg.op('dve', lambda e: e.tensor_scalar(yc[:], yc[:], sv[:, 13:14], sv[:, 14:15], ALU.mult, ALU.add), r=K(yc, sv), w=K(yc))
            pg.op('dve', lambda e: e.tensor_tensor(yc[:], yc[:], bv[:], ALU.add), r=K(yc, bv), w=K(yc))
            pg.op('dve', lambda e: e.tensor_tensor(y2[:], yc[:], gg[:], ALU.mult), r=K(yc, gg), w=K(y2))
            pg.dma('pool', d['YA'][:, t0:t0 + TT], y2[:], r=K(y2), w=["YA"])


PI = float(np.pi)


def phase_A5(pg, cfg, d, TS=2048):
    pg.barrier()
    NS, NTOK = cfg.NS, cfg.NTOK
    N = TS // 8
    L = int(np.log2(N))
    assert 2 ** L == N
    with contextlib.ExitStack() as es:
        ident = pg.sb("ident", [128, 128], F32, es)
        pg.dma('sp', ident[:], d['ident'], w=K(ident))
        s5v = pg.sb("s5v", [128, 3, 8], F32, es)
        pg.dma('sp', s5v[:], d['s5v'], w=K(s5v))
        bri = pg.sb("bri", [128, 2, 8, 16], F32, es)
        cri = pg.sb("cri", [128, 2, 8, 16], F32, es)
        pg.dma('sp', bri[:], d['s5b'], w=K(bri)); pg.dma('sp', cri[:], d['s5c'], w=K(cri))
        dsk = pg.sb("dsk", [128, 1], F32, es)
        pg.dma('sp', dsk[:], d['s5d'], w=K(dsk))
        sc = pg.sb("s5sc", [128, 24, 8], F32, es)
        def C(i): return sc[:, i, :]
        def tt(o, a, b, op): pg.op('dve', lambda e: e.tensor_tensor(o, a, b, op), r=K(sc, s5v), w=K(sc))
        def ts(o, a, s1, s2, op0, op1=None):
            if op1 is None:
                pg.op('dve', lambda e: e.tensor_scalar(o, a, s1, None, op0), r=K(sc, s5v), w=K(sc))
            else:
                pg.op('dve', lambda e: e.tensor_scalar(o, a, s1, s2, op0, op1), r=K(sc, s5v), w=K(sc))
        def act(o, a, f, **kw): pg.op('act', lambda e: e.activation(o, a, f, **kw), r=K(sc, s5v), w=K(sc))
        lr, li, ls = s5v[:, 0, :], s5v[:, 1, :], s5v[:, 2, :]
        STEP, LRS, ANG, MAG, A1, A2, SN, CS, AR, AI, DEN, NR, FR, FI, T0, T1, NAI, NFI = [C(i) for i in range(18)]
        act(STEP, ls, AF.Exp)
        tt(LRS, lr, STEP, ALU.mult); tt(ANG, li, STEP, ALU.mult)
        act(MAG, LRS, AF.Exp)
        kint = pg.sb("kint", [128, 8], I32, es)
        def rred(o, x):
            ts(T0, x, 1.0 / (2 * PI), None, ALU.mult)
            pg.op('dve', lambda e: e.tensor_copy(kint[:], T0), r=K(sc), w=K(kint))
            pg.op('dve', lambda e: e.tensor_copy(T1, kint[:]), r=K(kint), w=K(sc))
            pg.op('dve', lambda e: e.scalar_tensor_tensor(o, T1, -2 * PI, x, ALU.mult, ALU.add), r=K(sc), w=K(sc))
            ts(T0, o, PI, 2 * PI, ALU.is_gt, ALU.mult); tt(o, o, T0, ALU.subtract)
            ts(T0, o, -PI, 2 * PI, ALU.is_lt, ALU.mult); tt(o, o, T0, ALU.add)
        rred(A1, ANG)
        ts(A2, ANG, 0.5 * PI, None, ALU.add)
        rred(A2, A2)
        act(SN, A1, AF.Sin); act(CS, A2, AF.Sin)
        tt(AR, MAG, CS, ALU.mult); tt(AI, MAG, SN, ALU.mult)
        tt(DEN, lr, lr, ALU.mult); tt(T0, li, li, ALU.mult); tt(DEN, DEN, T0, ALU.add)
        pg.op('dve', lambda e: e.reciprocal(DEN, DEN), r=K(sc), w=K(sc))
        ts(NR, AR, -1.0, None, ALU.add)
        tt(T0, NR, lr, ALU.mult); tt(T1, AI, li, ALU.mult); tt(FR, T0, T1, ALU.add); tt(FR, FR, DEN, ALU.mult)
        tt(T0, AI, lr, ALU.mult); tt(T1, NR, li, ALU.mult); tt(FI, T0, T1, ALU.subtract); tt(FI, FI, DEN, ALU.mult)
        ts(NAI, AI, -1.0, None, ALU.mult); ts(NFI, FI, -1.0, None, ALU.mult)
        PR = pg.sb("PR", [128, 9, 8], F32, es); PIm = pg.sb("PIm", [128, 9, 8], F32, es); NPI = pg.sb("NPI", [128, 9, 8], F32, es)
        def tp(o, a, b, op): pg.op('dve', lambda e: e.tensor_tensor(o, a, b, op), r=K(sc, PR, PIm), w=K(PR, PIm, sc))
        tp(PR[:, 1, :], AR, AR, ALU.max); tp(PIm[:, 1, :], AI, AI, ALU.max)
        for j in range(1, 8):
            tp(T0, PR[:, j, :], AR, ALU.mult); tp(T1, PIm[:, j, :], AI, ALU.mult); tp(PR[:, j + 1, :], T0, T1, ALU.subtract)
            tp(T0, PR[:, j, :], AI, ALU.mult); tp(T1, PIm[:, j, :], AR, ALU.mult); tp(PIm[:, j + 1, :], T0, T1, ALU.add)
        pg.op('dve', lambda e: e.tensor_scalar(NPI[:], PIm[:], -1.0, None, ALU.mult), r=K(PIm), w=K(NPI))
        KR = pg.sb("KR", [128, L + 1, 8], F32, es); KI = pg.sb("KI", [128, L + 1, 8], F32, es); NKI = pg.sb("NKI", [128, L + 1, 8], F32, es)
        def tk(o, a, b, op): pg.op('dve', lambda e: e.tensor_tensor(o, a, b, op), r=K(sc, PR, PIm, KR, KI), w=K(KR, KI, sc))
        tk(KR[:, 0, :], PR[:, 8, :], PR[:, 8, :], ALU.max); tk(KI[:, 0, :], PIm[:, 8, :], PIm[:, 8, :], ALU.max)
        for l in range(L):
            tk(T0, KR[:, l, :], KR[:, l, :], ALU.mult); tk(T1, KI[:, l, :], KI[:, l, :], ALU.mult); tk(KR[:, l + 1, :], T0, T1, ALU.subtract)
            tk(T0, KR[:, l, :], KI[:, l, :], ALU.mult); tk(KI[:, l + 1, :], T0, T0, ALU.add)
        pg.op('dve', lambda e: e.tensor_scalar(NKI[:], KI[:], -1.0, None, ALU.mult), r=K(KI), w=K(NKI))
        bb = pg.sb("bb", [128, 2, 8, 16], F32, es)
        wide = pg.sb("wide", [128, 128], F32, es)
        BBp = pg.sb("BBp", [128, 2, 8, 128], F32, es)
        Cw = pg.sb("Cw", [128, 2, 8, 128], F32, es)
        pg.op('dve', lambda e: e.memset(Cw[:], 0.0), w=K(Cw))
        for dq in range(8):
            q = dq % 4
            fr, fi, nfi = FR[:, dq:dq + 1], FI[:, dq:dq + 1], NFI[:, dq:dq + 1]
            pg.op('dve', lambda e: e.tensor_scalar(bb[:, 0, dq, :], bri[:, 0, dq, :], fr, None, ALU.mult), r=K(bri, sc), w=K(bb))
            pg.op('dve', lambda e: e.scalar_tensor_tensor(bb[:, 0, dq, :], bri[:, 1, dq, :], nfi, bb[:, 0, dq, :], ALU.mult, ALU.add), r=K(bri, sc, bb), w=K(bb))
            pg.op('dve', lambda e: e.tensor_scalar(bb[:, 1, dq, :], bri[:, 1, dq, :], fr, None, ALU.mult), r=K(bri, sc), w=K(bb))
            pg.op('dve', lambda e: e.scalar_tensor_tensor(bb[:, 1, dq, :], bri[:, 0, dq, :], fi, bb[:, 1, dq, :], ALU.mult, ALU.add), r=K(bri, sc, bb), w=K(bb))
            for ri in range(2):
                pg.op('dve', lambda e: e.memset(wide[:], 0.0), w=K(wide))
                for gp in range(2):
                    c0 = (2 * q + gp) * 16
                    pg.op('dve', lambda e: e.tensor_copy(wide[64 * gp:64 * gp + 64, c0:c0 + 16], bb[64 * gp:64 * gp + 64, ri, dq, :]), r=K(bb), w=K(wide))
                    if ri == 0:
                        pg.op('dve', lambda e: e.tensor_copy(Cw[64 * gp:64 * gp + 64, 0, dq, c0:c0 + 16], cri[64 * gp:64 * gp + 64, 0, dq, :]), r=K(cri), w=K(Cw))
                    else:
                        pg.op('dve', lambda e: e.tensor_scalar(Cw[64 * gp:64 * gp + 64, 1, dq, c0:c0 + 16], cri[64 * gp:64 * gp + 64, 1, dq, :], -1.0, None, ALU.mult), r=K(cri), w=K(Cw))
                ps = pg.ps()
                pg.op('pe', lambda e: e.transpose(ps[:, 0:128], wide[:], ident[:]), r=K(wide, ident), w=K(ps))
                pg.op('act', lambda e: e.copy(BBp[:, ri, dq, :], ps[:, 0:128]), r=K(ps), w=K(BBp))
        u8 = [pg.sb("u8", [128, TS], F32, es) for _ in range(2)]
        X = [[pg.sb("X", [128, TS], F32, es) for _ in range(2)] for _ in range(4)]
        P = [[pg.sb("P", [128, N], F32, es) for _ in range(2)] for _ in range(2)]
        ys = [pg.sb("ys", [128, TS], F32, es) for _ in range(2)]
        carry = pg.sb("carry", [128, 2, 8, NS], F32, es)
        PT = d['PT']
        it = 0
        for dd in range(2):
            YS = d['YSF'] if dd == 0 else d['YSB']
            for b in range(NS):
                nseg = cfg.seqs[b] // TS
                for si in range(nseg):
                    seg = si if dd == 0 else nseg - 1 - si
                    t0 = cfg.off[b] + seg * TS
                    u = u8[it % 2]; yo = ys[it % 2]; it += 1
                    pg.dma('sp', u[:], PT[768:896, t0:t0 + TS], r=["PT"], w=K(u))
                    for q in range(4):
                        dq = dd * 4 + q
                        Xr, Xi = X[q]
                        for blk in range(TS // 512):
                            for ri, Xx in ((0, Xr), (1, Xi)):
                                ps = pg.ps()
                                pg.op('pe', lambda e: e.matmul(ps[:, :], lhsT=BBp[:, ri, dq, :], rhs=u[:, blk * 512:(blk + 1) * 512], start=True, stop=True),
                                      r=K(BBp, u), w=K(ps))
                                pg.op('act', lambda e: e.copy(Xx[:, blk * 512:(blk + 1) * 512], ps[:, :]), r=K(ps), w=K(Xx))
                        ar, ai, nai = AR[:, dq:dq + 1], AI[:, dq:dq + 1], NAI[:, dq:dq + 1]
                        def stt(o, a, s, bb_, rr, ww):
                            pg.op('dve', lambda e: e.scalar_tensor_tensor(o, a, s, bb_, ALU.mult, ALU.add), r=rr, w=ww)
                        if si > 0:
                            col = 0 if dd == 0 else TS - 1
                            cr_, ci_ = carry[:, 0, dq, b:b + 1], carry[:, 1, dq, b:b + 1]
                            stt(Xr[:, col:col + 1], cr_, ar, Xr[:, col:col + 1], K(carry, sc, Xr), K(Xr))
                            stt(Xr[:, col:col + 1], ci_, nai, Xr[:, col:col + 1], K(carry, sc, Xr), K(Xr))
                            stt(Xi[:, col:col + 1], ci_, ar, Xi[:, col:col + 1], K(carry, sc, Xi), K(Xi))
                            stt(Xi[:, col:col + 1], cr_, ai, Xi[:, col:col + 1], K(carry, sc, Xi), K(Xi))
                        Xr3 = Xr[:].rearrange("p (n j) -> p n j", j=8); Xi3 = Xi[:].rearrange("p (n j) -> p n j", j=8)
                        js = range(1, 8) if dd == 0 else range(6, -1, -1)
                        for j in js:
                            jp = j - 1 if dd == 0 else j + 1
                            stt(Xr3[:, :, j], Xr3[:, :, jp], ar, Xr3[:, :, j], K(Xr, sc), K(Xr))
                            stt(Xr3[:, :, j], Xi3[:, :, jp], nai, Xr3[:, :, j], K(Xr, Xi, sc), K(Xr))
                            stt(Xi3[:, :, j], Xi3[:, :, jp], ar, Xi3[:, :, j], K(Xi, sc), K(Xi))
                            stt(Xi3[:, :, j], Xr3[:, :, jp], ai, Xi3[:, :, j], K(Xr, Xi, sc), K(Xi))
                        je = 7 if dd == 0 else 0
                        cur = 0
                        pg.op('dve', lambda e: e.tensor_copy(P[0][0][:], Xr3[:, :, je]), r=K(Xr), w=K(P[0][0]))
                        pg.op('dve', lambda e: e.tensor_copy(P[0][1][:], Xi3[:, :, je]), r=K(Xi), w=K(P[0][1]))
                        for l in range(L):
                            sh = 2 ** l
                            sr, si_ = P[cur]; nr, ni = P[1 - cur]
                            kr, ki, nki = KR[:, l, dq:dq + 1], KI[:, l, dq:dq + 1], NKI[:, l, dq:dq + 1]
                            if dd == 0:
                                dst = slice(sh, N); src = slice(0, N - sh); keep = slice(0, sh)
                            else:
                                dst = slice(0, N - sh); src = slice(sh, N); keep = slice(N - sh, N)
                            stt(nr[:, dst], sr[:, src], kr, sr[:, dst], K(sr, KR), K(nr))
                            stt(nr[:, dst], si_[:, src], nki, nr[:, dst], K(si_, nr, NKI), K(nr))
                            stt(ni[:, dst], si_[:, src], kr, si_[:, dst], K(si_, KR), K(ni))
                            stt(ni[:, dst], sr[:, src], ki, ni[:, dst], K(sr, ni, KI), K(ni))
                            pg.op('dve', lambda e: e.tensor_copy(nr[:, keep], sr[:, keep]), r=K(sr), w=K(nr))
                            pg.op('dve', lambda e: e.tensor_copy(ni[:, keep], si_[:, keep]), r=K(si_), w=K(ni))
                            cur = 1 - cur
                        sr, si_ = P[cur]
                        ce = N - 1 if dd == 0 else 0
                        pg.op('dve', lambda e: e.tensor_copy(carry[:, 0, dq, b:b + 1], sr[:, ce:ce + 1]), r=K(sr), w=K(carry))
                        pg.op('dve', lambda e: e.tensor_copy(carry[:, 1, dq, b:b + 1], si_[:, ce:ce + 1]), r=K(si_), w=K(carry))
                        for j in range(8):
                            pj = j + 1 if dd == 0 else 8 - j
                            pr, pi_, npi = PR[:, pj, dq:dq + 1], PIm[:, pj, dq:dq + 1], NPI[:, pj, dq:dq + 1]
                            if dd == 0:
                                dst = slice(1, N); src = slice(0, N - 1)
                            else:
                                dst = slice(0, N - 1); src = slice(1, N)
                            if (dd == 0 and j == 7) or (dd == 1 and j == 0):
                                pg.op('dve', lambda e: e.tensor_copy(Xr3[:, :, j], sr[:]), r=K(sr), w=K(Xr))
                                pg.op('dve', lambda e: e.tensor_copy(Xi3[:, :, j], si_[:]), r=K(si_), w=K(Xi))
                                continue
                            stt(Xr3[:, dst, j], sr[:, src], pr, Xr3[:, dst, j], K(sr, PR, Xr), K(Xr))
                            stt(Xr3[:, dst, j], si_[:, src], npi, Xr3[:, dst, j], K(si_, NPI, Xr), K(Xr))
                            stt(Xi3[:, dst, j], si_[:, src], pr, Xi3[:, dst, j], K(si_, PR, Xi), K(Xi))
                            stt(Xi3[:, dst, j], sr[:, src], pi_, Xi3[:, dst, j], K(sr, PIm, Xi), K(Xi))
                    for blk in range(TS // 512):
                        ps = pg.ps()
                        for q in range(4):
                            dq = dd * 4 + q
                            for ri in range(2):
                                pg.op('pe', lambda e: e.matmul(ps[:, :], lhsT=Cw[:, ri, dq, :], rhs=X[q][ri][:, blk * 512:(blk + 1) * 512],
                                                               start=(q == 0 and ri == 0), stop=(q == 3 and ri == 1)), r=K(Cw, X[q][ri]), w=K(ps))
                        pg.op('act', lambda e: e.copy(yo[:, blk * 512:(blk + 1) * 512], ps[:, :]), r=K(ps), w=K(yo))
                    pg.dma('pool', YS[:, t0:t0 + TS], yo[:], r=K(yo), w=["YSF" if dd == 0 else "YSB"])
        TT = cfg.TT
        fb = [[pg.sb("f5", [128, TT], F32, es) for _ in range(3)] for _ in range(2)]
        y = pg.sb("y5", [128, TT], F32, es); y2 = pg.sb("y52", [128, TT], F32, es); z = [pg.sb("z5", [128, TT], F32, es) for _ in range(2)]
        for itt in range(NTOK // TT):
            t0 = itt * TT
            a, bq, uu = fb[itt % 2]; zz = z[itt % 2]
            pg.dma('sp', a[:], d['YSF'][:, t0:t0 + TT], r=["YSF"], w=K(a))
            pg.dma('sp', bq[:], d['YSB'][:, t0:t0 + TT], r=["YSB"], w=K(bq))
            pg.dma('sp', uu[:], PT[768:896, t0:t0 + TT], r=["PT"], w=K(uu))
            pg.op('dve', lambda e: e.tensor_tensor(y[:], a[:], bq[:], ALU.add), r=K(a, bq), w=K(y))
            pg.op('dve', lambda e: e.scalar_tensor_tensor(y[:], uu[:], dsk[:, 0:1], y[:], ALU.mult, ALU.add), r=K(uu, dsk, y), w=K(y))
            pg.op('dve', lambda e: e.tensor_tensor(y2[:], y[:], y[:], ALU.mult), r=K(y), w=K(y2))
            pg.op('dve', lambda e: e.tensor_scalar(y2[:], y2[:], 0.044715, 1.0, ALU.mult, ALU.add), r=K(y2), w=K(y2))
            pg.op('dve', lambda e: e.tensor_tensor(y2[:], y2[:], y[:], ALU.mult), r=K(y2, y), w=K(y2))
            pg.op('act', lambda e: e.activation(y2[:], y2[:], AF.Sigmoid, scale=1.5957691216057308), r=K(y2), w=K(y2))
            pg.op('dve', lambda e: e.tensor_tensor(zz[:], y[:], y2[:], ALU.mult), r=K(y, y2), w=K(zz))
            pg.dma('pool', d['ZZ'][:, t0:t0 + TT], zz[:], r=K(zz), w=["ZZ"])


def compute_mod_bc(pg, es, csT, adaw_d, adab_row_d, NS, out_d, key):
    o = pg.sb("modrow", [128, D], F32, es)
    bb = pg.sb("adabbc", [128, D], F32, es)
    pg.dma('sp', bb[:], bass.AP(adab_row_d.tensor, adab_row_d.offset, [[0, 128], [1, D]]), w=K(bb))
    csb = pg.sb("csb", [128, KC, 128], F32, es)
    w = pg.sb("wst", [128, KC, 512], F32, es)
    for b in range(NS):
        pg.op('dve', lambda e: e.tensor_copy(csb[:], csT[:, :, b:b + 1].to_broadcast([128, KC, 128])), r=K(csT), w=K(csb))
        for cb in range(4):
            pg.dma('sp', w[:], adaw_d[:, cb * 512:(cb + 1) * 512].rearrange("(kc p) f -> p kc f", p=128), w=K(w))
            ps = pg.ps()
            for kc in range(KC):
                pg.op('pe', lambda e: e.matmul(ps[:, :], lhsT=csb[:, kc, :], rhs=w[:, kc, :], start=(kc == 0), stop=(kc == KC - 1)), r=K(csb, w), w=K(ps))
            pg.op('dve', lambda e: e.tensor_tensor(o[:, cb * 512:(cb + 1) * 512], ps[:, :], bb[:, cb * 512:(cb + 1) * 512], ALU.add), r=K(ps, bb), w=K(o))
        pg.dma('pool', out_d[b:b + 1, :], o[0:1, :], r=K(o), w=[key])


def load_row_bc(pg, t, row_d, key):
    pg.dma('sp', t[:], bass.AP(row_d.tensor, row_d.offset, [[0, 128], [1, D]]), r=[key], w=K(t))


def norm_T_blocks(pg, x_d, blocks, b, ident, A_s, sh, hT, bufs, cnt):
    xb, xsb, ssb, jk = bufs
    for (r0, nr, c0) in blocks:
        n = cnt[0]; cnt[0] += 1
        xt = xb[n % len(xb)]; xs = xsb[n % len(xsb)]; ss = ssb[n % len(ssb)]
        pg.dma('sp', xt[0:nr, :], x_d[r0:r0 + nr, :], w=K(xt))
        pg.op('act', lambda e: e.activation(jk[0:nr, :], xt[0:nr, :], AF.Square, accum_out=ss[0:nr, 0:1]), r=K(xt), w=K(jk, ss))
        pg.op('act', lambda e: e.activation(ss[0:nr, 1:2], ss[0:nr, 0:1], AF.Sqrt, bias=EPS, scale=1.0 / D), r=K(ss), w=K(ss))
        pg.op('dve', lambda e: e.reciprocal(ss[0:nr, 2:3], ss[0:nr, 1:2]), r=K(ss), w=K(ss))
        pg.op('pool', lambda e: e.tensor_scalar(xt[0:nr, :], xt[0:nr, :], ss[0:nr, 2:3], None, ALU.mult), r=K(xt, ss), w=K(xt))
        for g in range(4):
            ps = pg.ps()
            for i in range(4):
                kc = g * 4 + i
                pg.op('pe', lambda e: e.transpose(ps[:, i * 128:i * 128 + nr], xt[0:nr, kc * 128:(kc + 1) * 128], ident[0:nr, 0:nr]),
                      r=K(xt, ident), w=K(ps))
            for i in range(4):
                kc = g * 4 + i
                if i % 2 == 0:
                    pg.op('dve', lambda e: e.tensor_scalar(hT[:, kc, c0:c0 + nr], ps[:, i * 128:i * 128 + nr],
                                                           A_s[:, kc, b:b + 1], sh[:, kc, b:b + 1], ALU.mult, ALU.add),
                          r=K(ps, A_s, sh), w=K(hT))
                else:
                    pg.op('act', lambda e: e.activation(hT[:, kc, c0:c0 + nr], ps[:, i * 128:i * 128 + nr], AF.Identity,
                                                        bias=sh[:, kc, b:b + 1], scale=A_s[:, kc, b:b + 1]),
                          r=K(ps, A_s, sh), w=K(hT))


def split_blocks(n, bs=128):
    return [(i, min(bs, n - i)) for i in range(0, n, bs)]


FF = 5632
NFC = 44


def phase_ffn(pg, cfg, d, segs, glu, final):
    pg.barrier()
    NS = len(segs)
    Lh = [s + 2 for s in segs]
    c0s = [int(v) for v in np.cumsum([0] + Lh[:-1])]
    o0s = [int(v) for v in np.cumsum([0] + list(segs[:-1]))]
    with contextlib.ExitStack() as es:
        ident = pg.sb("ident", [128, 128], F32, es)
        pg.dma('sp', ident[:], d['ident'], w=K(ident))
        mask = pg.sb("mask", [128, 2 * NS], F32, es)
        pg.dma('sp', mask[:], d['mask'], w=K(mask))
        mod2 = pg.sb("mod2", [128, 32, NS], F32, es)
        with contextlib.ExitStack() as es0:
            cT = pg.sb("cT", [128, KC, NS], F32, es0)
            pg.dma('sp', cT[:], d['cT'], w=K(cT))
            csT = pg.sb("csT", [128, KC, NS], F32, es0)
            pg.op('act', lambda e: e.activation(csT[:], cT[:], AF.Silu), r=K(cT), w=K(csT))
            compute_mod_bc(pg, es0, csT, d['adaw_g1'], d['adab_g1'], NS, d['G1R'], "G1R")
            compute_mod_bc(pg, es0, csT, d['adaw_g2'], d['adab_g2'], NS, d['G2R'], "G2R")
            compute_mod(pg, es0, d['cT'], d['adaw_m2'], d['adab_m2'], 32, NS, mod_tile=mod2)
        pg.barrier()
        with contextlib.ExitStack() as es1:
            g1t_ = pg.sb("g1bc", [128, D], F32, es1)
            g1bc = [g1t_] * NS
            wout = load_w_bf16(pg, es1, d['wout'], D, "wout")
            if glu:
                wglu = pg.sb("wglu", [128, 8, 1024], BF16, es1)
                stg = [pg.sb("wstg", [128, 1024], F32, es1) for _ in range(2)]
                for kc in range(8):
                    s_ = stg[kc % 2]
                    pg.dma('sp', s_[:], d['wglu'][kc * 128:(kc + 1) * 128, :], w=K(s_))
                    pg.op('dve', lambda e: e.tensor_copy(wglu[:, kc, :], s_[:]), r=K(s_), w=K(wglu))
                bglu = pg.sb("bglu", [128, 8], F32, es1)
                pg.dma('sp', bglu[:], d['bglu'], w=K(bglu))
            yin = [pg.sb("yin", [128, 16, 128], F32, es1) for _ in range(2)]
            yfb = [pg.sb("yfb", [128, 16, 128], BF16, es1) for _ in range(2)]
            zb = pg.sb("zb", [128, 8, 128], BF16, es1)
            sg = pg.sb("sg", [128, 128], F32, es1)
            xtk = [pg.sb("xtk", [128, D], F32, es1) for _ in range(2)]
            xmt = [pg.sb("xmt", [128, D], F32, es1) for _ in range(2)]
            YT = d['YT_in'].rearrange("c p t -> p c t")
            it = 0
            for b in range(NS):
                load_row_bc(pg, g1t_, d['G1R'][b:b + 1, :], "G1R")
                for (r, bs) in split_blocks(Lh[b]):
                    r0 = c0s[b] + r
                    yi = yin[it % 2]; yf = yfb[it % 2]; xt = xtk[it % 2]; xm = xmt[it % 2]; it += 1
                    pg.dma('sp', yi[:, :, 0:bs], YT[:, :, r0:r0 + bs], w=K(yi))
                    pg.dma('sp', xt[0:bs, :], d['x_in'][r0:r0 + bs, :], w=K(xt))
                    if glu:
                        pg.op('pool', lambda e: e.tensor_copy(yf[:, 0:8, 0:bs], yi[:, 0:8, 0:bs]), r=K(yi), w=K(yf))
                        pg.op('dve', lambda e: e.tensor_copy(zb[:, :, 0:bs], yi[:, 8:16, 0:bs]), r=K(yi), w=K(zb))
                        for oc in range(8):
                            ps = pg.ps()
                            for kc in range(8):
                                pg.op('pe', lambda e: e.matmul(ps[:, 0:bs], lhsT=wglu[:, kc, oc * 128:(oc + 1) * 128], rhs=zb[:, kc, 0:bs], start=(kc == 0), stop=(kc == 7)),
                                      r=K(wglu, zb), w=K(ps))
                            pg.op('act', lambda e: e.activation(sg[:, 0:bs], ps[:, 0:bs], AF.Sigmoid, bias=bglu[:, oc:oc + 1]), r=K(ps, bglu), w=K(sg))
                            pg.op('dve', lambda e: e.tensor_tensor(yf[:, 8 + oc, 0:bs], yi[:, 8 + oc, 0:bs], sg[:, 0:bs], ALU.mult), r=K(yi, sg), w=K(yf))
                    else:
                        pg.op('pool', lambda e: e.tensor_copy(yf[:, 0:8, 0:bs], yi[:, 0:8, 0:bs]), r=K(yi), w=K(yf))
                        pg.op('dve', lambda e: e.tensor_copy(yf[:, 8:16, 0:bs], yi[:, 8:16, 0:bs]), r=K(yi), w=K(yf))
                    for cb in range(4):
                        ps = pg.ps()
                        for kc in range(KC):
                            pg.op('pe', lambda e: e.matmul(ps[0:bs, :], lhsT=yf[:, kc, 0:bs], rhs=wout[:, kc, cb * 512:(cb + 1) * 512], start=(kc == 0), stop=(kc == KC - 1)),
                                  r=K(yf, wout), w=K(ps))
                        pg.op('dve', lambda e: e.tensor_tensor(xm[0:bs, cb * 512:(cb + 1) * 512], ps[0:bs, :], g1bc[b][0:bs, cb * 512:(cb + 1) * 512], ALU.mult),
                              r=K(ps, g1bc[b]), w=K(xm))
                        pg.op('pool', lambda e: e.tensor_tensor(xm[0:bs, cb * 512:(cb + 1) * 512], xm[0:bs, cb * 512:(cb + 1) * 512], xt[0:bs, cb * 512:(cb + 1) * 512], ALU.add),
                              r=K(xm, xt), w=K(xm))
                    pg.dma('pool', d['XM'][r0:r0 + bs, :], xm[0:bs, :], r=K(xm), w=["XM"])
        pg.barrier()
        g2t_ = pg.sb("g2bc", [128, D], F32, es)
        g2bc = [g2t_] * NS
        g2t = pg.sb("g2t", [128, KC], F32, es)
        pg.dma('sp', g2t[:], d['norm2_g'], w=K(g2t))
        A_s = pg.sb("A_s", [128, KC, NS], F32, es); sh = pg.sb("sh", [128, KC, NS], F32, es)
        for b in range(NS):
            pg.op('dve', lambda e: e.scalar_tensor_tensor(A_s[:, :, b], mod2[:, 16:32, b], 1.0, g2t[:, :], ALU.add, ALU.mult), r=K(mod2, g2t), w=K(A_s))
        pg.op('dve', lambda e: e.tensor_copy(sh[:], mod2[:, 0:16, :]), r=K(mod2), w=K(sh))
        cw = pg.sb("convw", [128, 3, 88], F32, es); cbias = pg.sb("convb", [128, 88], F32, es)
        pg.dma('sp', cw[:], d['convw'], w=K(cw)); pg.dma('sp', cbias[:], d['convb'], w=K(cbias))
        if final:
            fg = pg.sb("fgbc", [128, D], F32, es)
            pg.dma('sp', fg[:], bass.AP(d['final_g'].tensor, d['final_g'].offset, [[0, 128], [1, D]]), w=K(fg))
        bufs = norm_bufs(pg, es)
        hT = [pg.sb("hT2", [128, KC, 512], BF16, es) for _ in range(1)]
        wus = [pg.sb("wus", [128, KC, 128], F32, es) for _ in range(2)]
        wub = [pg.sb("wub", [128, KC, 128], BF16, es) for _ in range(2)]
        wds = [pg.sb("wds", [128, 1024], F32, es) for _ in range(2)]
        wdb = [pg.sb("wdb", [128, 1024], BF16, es) for _ in range(3)]
        aT = pg.sb("aT", [128, NFC, 512], BF16, es)
        cv = [pg.sb("cv", [128, 512], F32, es) for _ in range(2)]
        cg = [pg.sb("cg", [128, 512], F32, es) for _ in range(2)]
        xo = [pg.sb("xo", [128, D], F32, es) for _ in range(4)]
        tmpo = [pg.sb("tmpo", [128, 512], F32, es) for _ in range(2)]
        ssf = [pg.sb("ssf", [128, 4], F32, es) for _ in range(2)]
        cnt = [0]; wi = 0; fi = 0; di = 0; oi = 0
        wupv = d['wup']
        for b in range(NS):
            L = segs[b]
            wstart = 0
            load_row_bc(pg, g2t_, d['G2R'][b:b + 1, :], "G2R")
            while wstart < L:
                nout = min(510, L - wstart)
                ncol = nout + 2
                col0 = c0s[b] + wstart
                h = hT[0]; wi += 1
                blocks = [(col0 + r, n_, r) for (r, n_) in split_blocks(ncol)]
                norm_T_blocks(pg, d['XM'], blocks, b, ident, A_s, sh, h, bufs, cnt)
                if wstart == 0:
                    pg.op('dve', lambda e: e.tensor_scalar(h[:, :, 0:1], h[:, :, 0:1], mask[:, 2 * b:2 * b + 1], None, ALU.mult), r=K(h, mask), w=K(h))
                if wstart + nout == L:
                    pg.op('dve', lambda e: e.tensor_scalar(h[:, :, ncol - 1:ncol], h[:, :, ncol - 1:ncol], mask[:, 2 * b + 1:2 * b + 2], None, ALU.mult), r=K(h, mask), w=K(h))
                for fc in range(NFC):
                    pv = pg.ps(); pgt = pg.ps()
                    for (pp, coff) in ((pv, fc * 128), (pgt, FF + fc * 128)):
                        ws = wus[fi % 2]; wb = wub[fi % 2]; fi += 1
                        pg.dma('sp', ws[:], wupv[:, coff:coff + 128].rearrange("(kc p) f -> p kc f", p=128), w=K(ws))
                        pg.op('pool', lambda e: e.tensor_copy(wb[:], ws[:]), r=K(ws), w=K(wb))
                        for kc in range(KC):
                            pg.op('pe', lambda e: e.matmul(pp[:, 0:ncol], lhsT=wb[:, kc, :], rhs=h[:, kc, 0:ncol], start=(kc == 0), stop=(kc == KC - 1)), r=K(wb, h), w=K(pp))
                    c_v = cv[fc % 2]; c_g = cg[fc % 2]
                    for (pp, cc, col) in ((pv, c_v, fc), (pgt, c_g, NFC + fc)):
                        pg.op('dve', lambda e: e.tensor_scalar(cc[:, 0:nout], pp[:, 0:nout], cw[:, 0, col:col + 1], cbias[:, col:col + 1], ALU.mult, ALU.add), r=K(pp, cw, cbias), w=K(cc))
                        pg.op('dve', lambda e: e.scalar_tensor_tensor(cc[:, 0:nout], pp[:, 1:nout + 1], cw[:, 1, col:col + 1], cc[:, 0:nout], ALU.mult, ALU.add), r=K(pp, cw, cc), w=K(cc))
                        pg.op('dve', lambda e: e.scalar_tensor_tensor(cc[:, 0:nout], pp[:, 2:nout + 2], cw[:, 2, col:col + 1], cc[:, 0:nout], ALU.mult, ALU.add), r=K(pp, cw, cc), w=K(cc))
                    pg.op('act', lambda e: e.activation(c_g[:, 0:nout], c_g[:, 0:nout], AF.Silu), r=K(c_g), w=K(c_g))
                    pg.op('pool', lambda e: e.tensor_tensor(aT[:, fc, 0:nout], c_g[:, 0:nout], c_v[:, 0:nout], ALU.mult), r=K(c_g, c_v), w=K(aT))
                tblocks = split_blocks(nout)
                pss = {}
                for (tr, tn) in tblocks:
                    for cb in range(4):
                        pass
                for cbp in range(2):
                    accs = {}
                    for (tr, tn) in tblocks:
                        for cb in (2 * cbp, 2 * cbp + 1):
                            accs[(tr, cb)] = pg.ps()
                    for fc in range(NFC):
                        ws = wds[di % 2]; wb = wdb[di % 3]; di += 1
                        pg.dma('sp', ws[:, 0:1024], d['wdown'][fc * 128:(fc + 1) * 128, cbp * 1024:(cbp + 1) * 1024], w=K(ws))
                        pg.op('pool', lambda e: e.tensor_copy(wb[:, 0:1024], ws[:, 0:1024]), r=K(ws), w=K(wb))
                        for (tr, tn) in tblocks:
                            for j, cb in enumerate((2 * cbp, 2 * cbp + 1)):
                                ps = accs[(tr, cb)]
                                pg.op('pe', lambda e: e.matmul(ps[0:tn, :], lhsT=aT[:, fc, tr:tr + tn], rhs=wb[:, j * 512:(j + 1) * 512], start=(fc == 0), stop=(fc == NFC - 1)),
                                      r=K(aT, wb), w=K(ps))
                    for (tr, tn) in tblocks:
                        x_o = xo[(tr // 128) % 2] if False else None
                    for (tr, tn) in tblocks:
                        key = (b, wstart, tr)
                        if cbp == 0:
                            xoo = xo[oi % 4]; oi += 1
                            pss[key] = xoo
                            pg.dma('sp', xoo[0:tn, :], d['XM'][col0 + 1 + tr:col0 + 1 + tr + tn, :], r=["XM"], w=K(xoo))
                        xoo = pss[key]
                        for cb in (2 * cbp, 2 * cbp + 1):
                            ps = accs[(tr, cb)]
                            sl = slice(cb * 512, (cb + 1) * 512)
                            tp_ = tmpo[cb % 2]
                            pg.op('dve', lambda e: e.tensor_tensor(tp_[0:tn, :], ps[0:tn, :], g2bc[b][0:tn, sl], ALU.mult), r=K(ps, g2bc[b]), w=K(tp_))
                            pg.op('pool', lambda e: e.tensor_tensor(xoo[0:tn, sl], xoo[0:tn, sl], tp_[0:tn, :], ALU.add), r=K(xoo, tp_), w=K(xoo))
                        if cbp == 1:
                            orow = o0s[b] + wstart + tr
                            if final:
                                ss = ssf[oi % 2]; jk_ = bufs[3]
                                pg.op('act', lambda e: e.activation(jk_[0:tn, :], xoo[0:tn, :], AF.Square, accum_out=ss[0:tn, 0:1]), r=K(xoo), w=K(jk_, ss))
                                pg.op('act', lambda e: e.activation(ss[0:tn, 1:2], ss[0:tn, 0:1], AF.Sqrt, bias=EPS, scale=1.0 / D), r=K(ss), w=K(ss))
                                pg.op('dve', lambda e: e.reciprocal(ss[0:tn, 2:3], ss[0:tn, 1:2]), r=K(ss), w=K(ss))
                                pg.op('dve', lambda e: e.scalar_tensor_tensor(xoo[0:tn, :], xoo[0:tn, :], ss[0:tn, 2:3], fg[0:tn, :], ALU.mult, ALU.mult), r=K(xoo, ss, fg), w=K(xoo))
                            pg.dma('pool', d['X_out'][orow:orow + tn, :], xoo[0:tn, :], r=K(xoo), w=["X_out"])
                wstart += nout


SC_MLA = float(96 ** -0.5)
SC_D = float(64 ** -0.5)
SUBLN_EPS = 1e-5


def phase_C2(pg, cfg, d):
    pg.barrier()
    NS, TT = cfg.NS, cfg.TT
    PT = d['PT']
    with contextlib.ExitStack() as es:
        ident = pg.sb("ident", [128, 128], F32, es); pg.dma('sp', ident[:], d['ident'], w=K(ident))
        ones = pg.sb("ones", [128, 128], F32, es); pg.op('dve', lambda e: e.memset(ones[:], 1.0), w=K(ones))
        qg = pg.sb("qg", [128, 6], F32, es); pg.dma('sp', qg[:], d['qkv_g'], w=K(qg))
        rot = pg.sb("rot", [32, 32], F32, es); pg.dma('sp', rot[:], d['rot32'], w=K(rot))
        wuq = pg.sb("wuq", [128, 4, 96], BF16, es); wukv = pg.sb("wukv", [128, 2, 192], BF16, es)
        stq = pg.sb("stq", [128, 4, 96], F32, es); stk = pg.sb("stk", [128, 2, 192], F32, es)
        pg.dma('sp', stq[:], d['wuq'], w=K(stq)); pg.dma('sp', stk[:], d['wukv'], w=K(stk))
        pg.op('dve', lambda e: e.tensor_copy(wuq[:], stq[:]), r=K(stq), w=K(wuq))
        pg.op('dve', lambda e: e.tensor_copy(wukv[:], stk[:]), r=K(stk), w=K(wukv))
        raw = [pg.sb("rawc", [128, 10, TT], F32, es) for _ in range(2)]
        sq = pg.sb("sq", [128, TT], F32, es); rs = pg.sb("rs", [128, TT], F32, es)
        cqn = pg.sb("cqn", [128, 6, TT], BF16, es)
        tmpf = pg.sb("tmpf", [128, TT], F32, es)
        cs = [pg.sb("cs", [32, 2, TT], F32, es) for _ in range(2)]
        rr = pg.sb("rr", [32, TT], F32, es); rr2 = pg.sb("rr2", [32, TT], F32, es)
        ob = [pg.sb("ob", [128, 6, TT], BF16, es) for _ in range(2)]
        vb = [pg.sb("vbk", [128, 2, TT // 128, 128], BF16, es) for _ in range(2)]
        it = 0
        PTr = PT.rearrange("(c p) t -> p c t", p=128)
        for b in range(NS):
            for ti in range(cfg.seqs[b] // TT):
                t0 = cfg.off[b] + ti * TT; pos0 = ti * TT
                rw = raw[it % 2]; o = ob[it % 2]; v = vb[it % 2]; c_s = cs[it % 2]; it += 1
                pg.dma('sp', rw[:], PTr[:, :, t0:t0 + TT], r=["PT"], w=K(rw))
                pg.dma('sp', c_s[:], d['ropecs'][:, :, pos0:pos0 + TT], w=K(c_s))
                for (c0, c1, nf) in ((0, 4, 512.0), (4, 6, 256.0)):
                    ps = pg.ps()
                    for c in range(c0, c1):
                        pg.op('act', lambda e: e.activation(sq[:], rw[:, c, :], AF.Square), r=K(rw), w=K(sq))
                        pg.op('pe', lambda e: e.matmul(ps[:, 0:TT], lhsT=ones[:], rhs=sq[:], start=(c == c0), stop=(c == c1 - 1)), r=K(ones, sq), w=K(ps))
                    pg.op('act', lambda e: e.activation(rs[:], ps[:, 0:TT], AF.Sqrt, bias=EPS, scale=1.0 / nf), r=K(ps), w=K(rs))
                    pg.op('dve', lambda e: e.reciprocal(rs[:], rs[:]), r=K(rs), w=K(rs))
                    for c in range(c0, c1):
                        pg.op('dve', lambda e: e.scalar_tensor_tensor(cqn[:, c, :], rw[:, c, :], qg[:, c:c + 1], rs[:], ALU.mult, ALU.mult), r=K(rw, qg, rs), w=K(cqn))
                ps = pg.ps()
                for c in range(4):
                    pg.op('pe', lambda e: e.matmul(ps[0:64, 0:TT], lhsT=wuq[:, c, 0:64], rhs=cqn[:, c, :], start=(c == 0), stop=(c == 3)), r=K(wuq, cqn), w=K(ps))
                pg.op('act', lambda e: e.copy(o[0:64, 0, :], ps[0:64, 0:TT]), r=K(ps), w=K(o))
                ps = pg.ps()
                for c in range(4):
                    pg.op('pe', lambda e: e.matmul(ps[0:32, 0:TT], lhsT=wuq[:, c, 64:96], rhs=cqn[:, c, :], start=(c == 0), stop=(c == 3)), r=K(wuq, cqn), w=K(ps))
                def rope(src_ap, src_keys, dst):
                    pg.op('act', lambda e: e.copy(rr[:], src_ap), r=src_keys, w=K(rr))
                    ps2 = pg.ps()
                    pg.op('pe', lambda e: e.matmul(ps2[0:32, 0:TT], lhsT=rot[:], rhs=rr[:], start=True, stop=True), r=K(rot, rr), w=K(ps2))
                    pg.op('dve', lambda e: e.tensor_tensor(rr2[:], ps2[0:32, 0:TT], c_s[:, 1, :], ALU.mult), r=K(ps2, c_s), w=K(rr2))
                    pg.op('dve', lambda e: e.tensor_tensor(rr[:], rr[:], c_s[:, 0, :], ALU.mult), r=K(rr, c_s), w=K(rr))
                    pg.op('dve', lambda e: e.tensor_tensor(dst, rr[:], rr2[:], ALU.add), r=K(rr, rr2), w=K(o))
                rope(ps[0:32, 0:TT], K(ps), o[0:32, 1, :])
                rope(rw[0:32, 6, :], K(rw), o[0:32, 3, :])
                ps = pg.ps()
                for c in range(2):
                    pg.op('pe', lambda e: e.matmul(ps[0:64, 0:TT], lhsT=wukv[:, c, 0:64], rhs=cqn[:, 4 + c, :], start=(c == 0), stop=(c == 1)), r=K(wukv, cqn), w=K(ps))
                pg.op('act', lambda e: e.copy(o[0:64, 2, :], ps[0:64, 0:TT]), r=K(ps), w=K(o))
                pg.op('pool', lambda e: e.tensor_copy(o[:, 4, :], rw[:, 7, :]), r=K(rw), w=K(o))
                pg.op('pool', lambda e: e.tensor_copy(o[:, 5, :], rw[:, 8, :]), r=K(rw), w=K(o))
                for j in range(TT // 128):
                    ps = pg.ps()
                    for c in range(2):
                        pg.op('pe', lambda e: e.matmul(ps[:, 0:128], lhsT=cqn[:, 4 + c, j * 128:(j + 1) * 128], rhs=wukv[:, c, 64:192], start=(c == 0), stop=(c == 1)), r=K(cqn, wukv), w=K(ps))
                    pg.op('pe', lambda e: e.transpose(ps[:, 128:256], rw[:, 9, j * 128:(j + 1) * 128], ident[:]), r=K(rw, ident), w=K(ps))
                    pg.op('act', lambda e: e.copy(v[:, 0, j, :], ps[:, 0:128]), r=K(ps), w=K(v))
                    pg.op('dve', lambda e: e.tensor_copy(v[:, 1, j, :], ps[:, 128:256]), r=K(ps), w=K(v))
                pg.dma('pool', d['QN'][:, t0:t0 + TT], o[0:64, 0, :], r=K(o), w=["QN"])
                pg.dma('pool', d['QR'][:, t0:t0 + TT], o[0:32, 1, :], r=K(o), w=["QR"])
                pg.dma('pool', d['KN'][:, t0:t0 + TT], o[0:64, 2, :], r=K(o), w=["KN"])
                pg.dma('pool', d['KR'][:, t0:t0 + TT], o[0:32, 3, :], r=K(o), w=["KR"])
                pg.dma('pool', d['DQ'][:, t0:t0 + TT], o[:, 4, :], r=K(o), w=["DQ"])
                pg.dma('pool', d['DK'][:, t0:t0 + TT], o[:, 5, :], r=K(o), w=["DK"])
                pg.dma('pool', d['VTK'][t0:t0 + TT, :].rearrange("(j p) f -> p j f", p=128), v[:, 0, :, :], r=K(v), w=["VTK"])
                pg.dma('pool', d['DVT'][t0:t0 + TT, :].rearrange("(j p) f -> p j f", p=128), v[:, 1, :, :], r=K(v), w=["DVT"])


def phase_C3(pg, cfg, d, lambda_init, KB=2048):
    pg.barrier()
    NS = cfg.NS
    QT = 512
    with contextlib.ExitStack() as es:
        onesb = pg.sb("onesb", [128, 128], BF16, es); pg.op('dve', lambda e: e.memset(onesb[:], 1.0), w=K(onesb))
        onesf = pg.sb("onesf", [128, 128], F32, es); pg.op('dve', lambda e: e.memset(onesf[:], 1.0), w=K(onesf))
        lq = pg.sb("lq", [128, 4, 64], F32, es); pg.dma('sp', lq[:], d['lqk'], w=K(lq))
        lt = pg.sb("lt", [128, 64], F32, es); lam = pg.sb("lam", [128, 4], F32, es)
        for i in range(2):
            pg.op('dve', lambda e: e.tensor_tensor(lt[:], lq[:, 2 * i, :], lq[:, 2 * i + 1, :], ALU.mult), r=K(lq), w=K(lt))
            pg.op('dve', lambda e: e.tensor_reduce(lam[:, i:i + 1], lt[:], AX.X, ALU.add), r=K(lt), w=K(lam))
        pg.op('act', lambda e: e.activation(lam[:, 0:2], lam[:, 0:2], AF.Exp), r=K(lam), w=K(lam))
        pg.op('dve', lambda e: e.tensor_tensor(lam[:, 2:3], lam[:, 1:2], lam[:, 0:1], ALU.subtract), r=K(lam), w=K(lam))
        pg.op('dve', lambda e: e.tensor_scalar(lam[:, 3:4], lam[:, 2:3], -lambda_init, None, ALU.add), r=K(lam), w=K(lam))
        nlam = lam[:, 3:4]
        sg_ = pg.sb("sublng", [128, 1], F32, es); pg.dma('sp', sg_[:], d['subln_g'], w=K(sg_))
        relb = pg.sb("relb", [32, 1], F32, es); pg.dma('sp', relb[:], d['relb'], w=K(relb))
        oh = pg.sb("oh", [32, 1536], F32, es); pg.dma('sp', oh[:], d['ohrev'], w=K(oh))
        vrow = pg.sb("vrow", [1, 1536], F32, es)
        for blk in range(3):
            ps = pg.ps()
            pg.op('pe', lambda e: e.matmul(ps[0:1, :], lhsT=relb[:, 0:1], rhs=oh[:, blk * 512:(blk + 1) * 512], start=True, stop=True), r=K(relb, oh), w=K(ps))
            pg.op('act', lambda e: e.copy(vrow[0:1, blk * 512:(blk + 1) * 512], ps[0:1, :]), r=K(ps), w=K(vrow))
        pg.dma('pool', d['VREV'][0:1, :], vrow[0:1, :], r=K(vrow), w=["VREV"])
        NEAR = [-128, 0, 128, 256, 384, 512]
        btile = pg.sb("btile", [128, 6, QT], F32, es)
        for i, dlt in enumerate(NEAR):
            for kq in range(128):
                st = 639 - dlt - kq
                pg.dma('sp' if kq % 2 == 0 else 'act', btile[kq:kq + 1, i, :], d['VREV'][0:1, st:st + QT], r=["VREV"], w=K(btile))
        bfar = pg.sb("bfar", [128, 2], F32, es)
        pg.dma('sp', bfar[:], d['bfar'], w=K(bfar))
        q_t = [pg.sb("q_t", [128, 4, QT], BF16, es) for _ in range(2)]
        k_t = [pg.sb("k_t", [128, 3, KB], BF16, es) for _ in range(2)]
        v_t = [pg.sb("v_t", [128, 2, KB // 128, 128], BF16, es) for _ in range(2)]
        pT = [pg.sb("pT", [128, QT], BF16, es) for _ in range(3)]
        tb = [pg.sb("tb", [128, QT], F32, es) for _ in range(2)]
        fin = [pg.sb("fin", [128, QT], F32, es) for _ in range(4)]
        qi = 0; ki = 0; pi = 0
        for b in range(NS):
            S = cfg.seqs[b]; o0 = cfg.off[b]
            for qt in range(S // QT):
                q0 = qt * QT
                q = q_t[qi % 2]; qi += 1
                pg.dma('sp', q[0:64, 0, :], d['QN'][:, o0 + q0:o0 + q0 + QT], r=["QN"], w=K(q))
                pg.dma('sp', q[0:32, 1, :], d['QR'][:, o0 + q0:o0 + q0 + QT], r=["QR"], w=K(q))
                pg.dma('sp', q[:, 2, :], d['DQ'][:, o0 + q0:o0 + q0 + QT], r=["DQ"], w=K(q))
                acc = [pg.ps_at(i_) for i_ in range(6)]
                sci = 0
                nk = S // 128
                for kb in range(S // KB):
                    kk_ = k_t[ki % 2]; vv = v_t[ki % 2]; ki += 1
                    kt0 = o0 + kb * KB
                    pg.dma('sp', kk_[0:64, 0, :], d['KN'][:, kt0:kt0 + KB], r=["KN"], w=K(kk_))
                    pg.dma('sp', kk_[0:32, 1, :], d['KR'][:, kt0:kt0 + KB], r=["KR"], w=K(kk_))
                    pg.dma('sp', kk_[:, 2, :], d['DK'][:, kt0:kt0 + KB], r=["DK"], w=K(kk_))
                    pg.dma('sp', vv[:, 0, :, :], d['VTK'][kt0:kt0 + KB, :].rearrange("(j p) f -> p j f", p=128), r=["VTK"], w=K(vv))
                    pg.dma('sp', vv[:, 1, :, :], d['DVT'][kt0:kt0 + KB, :].rearrange("(j p) f -> p j f", p=128), r=["DVT"], w=K(vv))
                    for kj in range(KB // 128):
                        kidx = kb * (KB // 128) + kj
                        k0 = kidx * 128
                        first = (kidx == 0); last = (kidx == nk - 1)
                        ksl = slice(kj * 128, (kj + 1) * 128)
                        ps = pg.ps_at(6 + sci % 2); sci += 1
                        pg.op('pe', lambda e: e.matmul(ps[:, :], lhsT=kk_[0:64, 0, ksl], rhs=q[0:64, 0, :], start=True, stop=False), r=K(kk_, q), w=K(ps))
                        pg.op('pe', lambda e: e.matmul(ps[:, :], lhsT=kk_[0:32, 1, ksl], rhs=q[0:32, 1, :], start=False, stop=True), r=K(kk_, q), w=K(ps))
                        p_ = pT[pi % 3]; pi += 1
                        pg.op('act', lambda e: e.activation(p_[:], ps[:, :], AF.Exp, scale=SC_MLA), r=K(ps), w=K(p_))
                        pg.op('pe', lambda e: e.matmul(acc[0][:, :], lhsT=vv[:, 0, kj, :], rhs=p_[:], start=first, stop=last), r=K(vv, p_), w=K(acc[0]))
                        pg.op('pe', lambda e: e.matmul(acc[1][:, :], lhsT=onesb[:], rhs=p_[:], start=first, stop=last), r=K(onesb, p_), w=K(acc[1]))
                        dlt = k0 - q0
                        for m in range(2):
                            ps = pg.ps_at(6 + sci % 2); sci += 1
                            pg.op('pe', lambda e: e.matmul(ps[:, :], lhsT=kk_[64 * m:64 * m + 64, 2, ksl], rhs=q[64 * m:64 * m + 64, 2, :], start=True, stop=True), r=K(kk_, q), w=K(ps))
                            p_ = pT[pi % 3]; pi += 1
                            if dlt in NEAR:
                                t_ = tb[m]
                                pg.op('dve', lambda e: e.scalar_tensor_tensor(t_[:], ps[:, :], SC_D, btile[:, NEAR.index(dlt), :], ALU.mult, ALU.add), r=K(ps, btile), w=K(t_))
                                pg.op('act', lambda e: e.activation(p_[:], t_[:], AF.Exp), r=K(t_), w=K(p_))
                            else:
                                col = 1 if dlt > 0 else 0
                                pg.op('act', lambda e: e.activation(p_[:], ps[:, :], AF.Exp, scale=SC_D, bias=bfar[:, col:col + 1]), r=K(ps, bfar), w=K(p_))
                            pg.op('pe', lambda e: e.matmul(acc[2 + 2 * m][:, :], lhsT=vv[:, 1, kj, :], rhs=p_[:], start=first, stop=last), r=K(vv, p_), w=K(acc[2 + 2 * m]))
                            pg.op('pe', lambda e: e.matmul(acc[3 + 2 * m][:, :], lhsT=onesb[:], rhs=p_[:], start=first, stop=last), r=K(onesb, p_), w=K(acc[3 + 2 * m]))
                f0, f1, f2, f3 = fin
                pg.op('dve', lambda e: e.reciprocal(f0[:], acc[1][:, :]), r=K(acc[1]), w=K(f0))
                pg.op('dve', lambda e: e.tensor_tensor(f1[:], acc[0][:, :], f0[:], ALU.mult), r=K(acc[0], f0), w=K(f1))
                pg.dma('pool', d['YC'][:, o0 + q0:o0 + q0 + QT], f1[:], r=K(f1), w=["YC"])
                pg.op('dve', lambda e: e.reciprocal(f0[:], acc[3][:, :]), r=K(acc[3]), w=K(f0))
                pg.op('dve', lambda e: e.tensor_tensor(f2[:], acc[2][:, :], f0[:], ALU.mult), r=K(acc[2], f0), w=K(f2))
                pg.op('dve', lambda e: e.reciprocal(f0[:], acc[5][:, :]), r=K(acc[5]), w=K(f0))
                pg.op('dve', lambda e: e.tensor_tensor(f3[:], acc[4][:, :], f0[:], ALU.mult), r=K(acc[4], f0), w=K(f3))
                pg.op('dve', lambda e: e.scalar_tensor_tensor(f2[:], f3[:], nlam, f2[:], ALU.mult, ALU.add), r=K(f3, lam, f2), w=K(f2))
                pg.op('act', lambda e: e.activation(f3[:], f2[:], AF.Square), r=K(f2), w=K(f3))
                ps = pg.ps_at(6)
                pg.op('pe', lambda e: e.matmul(ps[:, :], lhsT=onesf[:], rhs=f3[:], start=True, stop=True), r=K(onesf, f3), w=K(ps))
                pg.op('act', lambda e: e.activation(f0[:], ps[:, :], AF.Sqrt, bias=SUBLN_EPS, scale=1.0 / 128), r=K(ps), w=K(f0))
                pg.op('dve', lambda e: e.reciprocal(f0[:], f0[:]), r=K(f0), w=K(f0))
                pg.op('dve', lambda e: e.tensor_tensor(f2[:], f2[:], f0[:], ALU.mult), r=K(f2, f0), w=K(f2))
                pg.op('dve', lambda e: e.tensor_scalar(f3[:], f2[:], sg_[:, 0:1], 1.0 - lambda_init, ALU.mult, ALU.mult), r=K(f2, sg_), w=K(f3))
                pg.dma('pool', d['YD'][:, o0 + q0:o0 + q0 + QT], f3[:], r=K(f3), w=["YD"])


import numpy as np, math
def t5_bucket_np(rel):
    half = 16; max_exact = 8
    n = np.abs(rel)
    with np.errstate(divide='ignore'):
        large = max_exact + (np.log(np.maximum(n, 1).astype(np.float32) / np.float32(max_exact)) / np.float32(math.log(128 / max_exact)) * np.float32(half - max_exact)).astype(np.int32)
    large = np.minimum(large, half - 1)
    return np.where(rel > 0, half, 0) + np.where(n < max_exact, n, large)
def ohrev():
    j = np.arange(1536); rel = 639 - j
    bk = t5_bucket_np(rel)
    oh = np.zeros((32, 1536), np.float32); oh[bk, j] = 1.0
    return oh
def rot32():
    r = np.zeros((32, 32), np.float32)
    for i in range(16):
        r[16 + i, i] = -1.0; r[i, 16 + i] = 1.0
    return r
def ropecs(S):
    inv = (1.0 / (np.float32(10000.0) ** (np.arange(0, 32, 2, dtype=np.float32) / np.float32(32)))).astype(np.float32)
    ang = np.arange(S, dtype=np.float32)[:, None] * inv[None, :]
    c = np.cos(ang).astype(np.float32).T; s_ = np.sin(ang).astype(np.float32).T
    out = np.zeros((32, 2, S), np.float32)
    out[:16, 0] = c; out[16:, 0] = c; out[:16, 1] = s_; out[16:, 1] = s_
    return out
def colsC(h):
    return [np.arange(0, 512), np.arange(512, 768), np.arange(768, 800), 800 + 128 * h + np.arange(128), 1824 + 128 * h + np.arange(128), 2848 + 128 * h + np.arange(128)]
def attn_small(inp, h, j=0):
    wuq = inp['mla_w_uq'][j][:, 96 * h:96 * h + 96].reshape(4, 128, 96).transpose(1, 0, 2)
    wukv = inp['mla_w_ukv'][j][:, 192 * h:192 * h + 192].reshape(2, 128, 192).transpose(1, 0, 2)
    qkv_g = np.concatenate([inp['mla_q_norm_g'][j].reshape(4, 128).T, inp['mla_kv_norm_g'][j].reshape(2, 128).T], 1)
    lqk = np.stack([inp['diff_lq1'][j], inp['diff_lk1'][j], inp['diff_lq2'][j], inp['diff_lk2'][j]], 0)[None].repeat(128, 0)
    rb = inp['rel_bias']
    return dict(wuq=np.ascontiguousarray(wuq, np.float32), wukv=np.ascontiguousarray(wukv, np.float32), qkv_g=np.ascontiguousarray(qkv_g, np.float32),
                lqk=np.ascontiguousarray(lqk, np.float32), relb=np.ascontiguousarray(rb[:, h:h + 1]), bfar=np.ascontiguousarray(np.stack([np.full(128, rb[15, h]), np.full(128, rb[31, h])], 1).astype(np.float32)),
                subln_g=np.ascontiguousarray(inp['diff_subln_g'][j].reshape(128, 1)))


import math as _math
from concourse.bass_utils import run_bass_kernel_spmd

NCORE = 8
A_COLS_ = 3456


def _din(nc, name, shape):
    return nc.dram_tensor(name, list(shape), F32, kind="ExternalInput").ap()


def _dint(nc, name, shape, dt=F32):
    return nc.dram_tensor(name, list(shape), dt, kind="Internal").ap()


def _dout(nc, name, shape):
    return nc.dram_tensor(name, list(shape), F32, kind="ExternalOutput").ap()


def _colsA(vc):
    r = np.arange(128 * vc, 128 * vc + 128)
    return np.concatenate([r, 1024 + r, 2048 + r, np.arange(3072, 3456), A_COLS_ + r])


def _rwkv_small(inp, vc):
    sl = slice(128 * vc, 128 * vc + 128)
    mu = inp['rwkv_mu'][0]
    sv = np.zeros((128, 24), np.float32)
    sv[:, 0] = mu[0:1024][sl]; sv[:, 1] = mu[1024:2048][sl]; sv[:, 2] = mu[2048:3072][sl]
    sv[:, 3] = mu[3072:3200]; sv[:, 4] = mu[3200:3328]; sv[:, 5] = mu[3328:3456]
    for dd in range(2):
        sv[:, 6 + dd] = inp['rwkv_w0'][0][dd][sl]; sv[:, 8 + dd] = inp['rwkv_a0'][0][dd][sl]
    sv[:, 10] = inp['rwkv_k_k'][0][sl]; sv[:, 11] = inp['rwkv_k_a'][0][sl]; sv[:, 12] = inp['rwkv_r_k'][0].reshape(-1)[sl]
    sv[:, 13] = inp['rwkv_lnx_g'][0][sl]; sv[:, 14] = inp['rwkv_lnx_b'][0][sl]
    wup = np.concatenate([inp['rwkv_w_up'][0][0][:, sl], inp['rwkv_w_up'][0][1][:, sl]], 0)
    aup = np.concatenate([inp['rwkv_a_up'][0][0][:, sl], inp['rwkv_a_up'][0][1][:, sl]], 0)
    gup = inp['rwkv_g_up'][0][:, sl]
    return dict(rwkv_sv=sv, wup=np.ascontiguousarray(wup), aup=np.ascontiguousarray(aup), gup=np.ascontiguousarray(gup))


def _s5_small(inp, vc):
    s5v = np.zeros((128, 3, 8), np.float32); s5b = np.zeros((128, 2, 8, 16), np.float32); s5c = np.zeros((128, 2, 8, 16), np.float32)
    for dd in range(2):
        for q in range(4):
            for gp in range(2):
                g = 8 * vc + 2 * q + gp; dq = dd * 4 + q; sl = slice(64 * gp, 64 * gp + 64)
                s5v[sl, 0, dq] = inp['s5_lam_re'][0, dd, g]; s5v[sl, 1, dq] = inp['s5_lam_im'][0, dd, g]; s5v[sl, 2, dq] = inp['s5_log_step'][0, dd, g]
                s5b[sl, 0, dq, :] = inp['s5_b_re'][0, dd, g]; s5b[sl, 1, dq, :] = inp['s5_b_im'][0, dd, g]
                s5c[sl, 0, dq, :] = inp['s5_c_re'][0, dd, g].T; s5c[sl, 1, dq, :] = inp['s5_c_im'][0, dd, g].T
    return dict(s5v=s5v, s5b=s5b, s5c=s5c, s5d=np.ascontiguousarray(inp['s5_d'][0][128 * vc:128 * vc + 128].reshape(128, 1)))


def _halo_rows(cfg, segs, arr, c):
    out = []
    for b in range(cfg.NS):
        S = cfg.seqs[b]; sg = segs[b]; lo = c * sg - 1; hi = (c + 1) * sg + 1
        blk = np.zeros((sg + 2, arr.shape[1]), np.float32)
        a = max(lo, 0); e = min(hi, S)
        blk[a - lo:e - lo] = arr[cfg.off[b] + a:cfg.off[b] + e]
        out.append(blk)
    return np.concatenate(out, 0)


def _halo_cols_T(cfg, segs, arrT, c):
    out = []
    for b in range(cfg.NS):
        S = cfg.seqs[b]; sg = segs[b]; lo = c * sg - 1; hi = (c + 1) * sg + 1
        blk = np.zeros((arrT.shape[0], sg + 2), np.float32)
        a = max(lo, 0); e = min(hi, S)
        blk[:, a - lo:e - lo] = arrT[:, cfg.off[b] + a:cfg.off[b] + e]
        out.append(blk)
    return np.concatenate(out, 1)


def _run_A(cfg, inp, x_all, cT, TS):
    NS, NTOK = cfg.NS, cfg.NTOK
    nc = bass.Bass("TRN2", target_bir_lowering=False)
    d = {}
    shapes = dict(ident=[128, 128], bones=[128, 128], cT=[128, 16, NS], adaw_A=[2048, 4096], adab_A=[128, 32], norm1_g0=[128, 16], w_inA=[2048, 896],
                  x_all=[NTOK, 2048], rwkv_sv=[128, 24], wup=[128, 128], aup=[128, 128], gup=[128, 128], s5v=[128, 3, 8], s5b=[128, 2, 8, 16], s5c=[128, 2, 8, 16], s5d=[128, 1])
    for k, v in shapes.items():
        d[k] = _din(nc, k, v)
    d['PT'] = _dint(nc, "PT", [896, NTOK]); d['VT'] = _dint(nc, "VT", [10, 2, NTOK, 64])
    for n in ['MV', 'BV', 'GG', 'YF', 'YB', 'YSF', 'YSB']:
        d[n] = _dint(nc, n, [128, NTOK])
    d['YA'] = _dout(nc, "YA", [128, NTOK]); d['ZZ'] = _dout(nc, "ZZ", [128, NTOK])
    with contextlib.ExitStack() as es:
        pg = Prog(nc, es); pg.psum_pool(8)
        phase_A1(pg, cfg, d)
        phase_A5(pg, cfg, d, TS=TS)
        phase_A2(pg, cfg, d)
        phase_A3(pg, cfg, d)
        phase_A4(pg, cfg, d)
        pg.finish()
    bones = np.kron(np.eye(2), np.ones((64, 64))).astype(np.float32)
    aw = np.ascontiguousarray(inp['ada_w'][0][:, :4096]); ab = lay_vec(inp['ada_b'][0][:4096]); g1 = lay_vec(inp['norm1_g'][0])
    ident = np.eye(128, dtype=np.float32)
    maps = []
    for vc in range(NCORE):
        m = dict(ident=ident, bones=bones, cT=cT, adaw_A=aw, adab_A=ab, norm1_g0=g1, w_inA=np.ascontiguousarray(inp['even_w_in'][0][:, _colsA(vc)]), x_all=x_all)
        m.update(_rwkv_small(inp, vc)); m.update(_s5_small(inp, vc)); maps.append(m)
    res = run_bass_kernel_spmd(nc, maps, core_ids=list(range(NCORE)))
    YT = np.zeros((2048, NTOK), np.float32)
    for vc in range(NCORE):
        YT[128 * vc:128 * vc + 128] = res.results[vc]["YA"]
        YT[1024 + 128 * vc:1024 + 128 * vc + 128] = res.results[vc]["ZZ"]
    return YT


def _run_ffn(cfg, inp, layer, YT, x_all, cT, glu, final):
    NS, NTOK = cfg.NS, cfg.NTOK
    segs = [s // NCORE for s in cfg.seqs]
    NTBH = sum(s + 2 for s in segs); NTB = sum(segs)
    nc = bass.Bass("TRN2", target_bir_lowering=False)
    d = {}
    shapes = dict(ident=[128, 128], YT_in=[16, 128, NTBH], x_in=[NTBH, 2048], cT=[128, 16, NS], adaw_g1=[2048, 2048], adab_g1=[1, 2048], adaw_m2=[2048, 4096], adab_m2=[128, 32],
                  adaw_g2=[2048, 2048], adab_g2=[1, 2048], norm2_g=[128, 16], wout=[2048, 2048], wup=[2048, 11264], convw=[128, 3, 88], convb=[128, 88], wdown=[5632, 2048], mask=[128, 2 * NS])
    if glu:
        shapes.update(wglu=[1024, 1024], bglu=[128, 8])
    if final:
        shapes.update(final_g=[1, 2048])
    for k, v in shapes.items():
        d[k] = _din(nc, k, v)
    d['XM'] = _dint(nc, "XM", [NTBH, 2048]); d['G1R'] = _dint(nc, "G1R", [NS, 2048]); d['G2R'] = _dint(nc, "G2R", [NS, 2048])
    d['X_out'] = _dout(nc, "X_out", [NTB, 2048])
    with contextlib.ExitStack() as es:
        pg = Prog(nc, es); pg.psum_pool(8)
        phase_ffn(pg, cfg, d, segs, glu=glu, final=final)
        pg.finish()
    i = layer
    aw = inp['ada_w'][i]; ab = inp['ada_b'][i]
    base = dict(ident=np.eye(128, dtype=np.float32), cT=cT,
                adaw_g1=np.ascontiguousarray(aw[:, 4096:6144]), adab_g1=ab[4096:6144].reshape(1, -1).copy(), adaw_m2=np.ascontiguousarray(aw[:, 6144:10240]), adab_m2=lay_vec(ab[6144:10240]),
                adaw_g2=np.ascontiguousarray(aw[:, 10240:12288]), adab_g2=ab[10240:12288].reshape(1, -1).copy(), norm2_g=lay_vec(inp['norm2_g'][i]),
                wout=np.ascontiguousarray(inp['even_w_out'][0] if layer == 0 else inp['odd_w_out'][0]),
                wup=np.ascontiguousarray(inp['ffn_w_up'][i]), convw=np.ascontiguousarray(inp['ffn_conv_w'][i].reshape(3, 88, 128).transpose(2, 0, 1)), convb=lay_vec(inp['ffn_conv_b'][i]),
                wdown=np.ascontiguousarray(inp['ffn_w_down'][i]))
    if glu:
        base.update(wglu=np.ascontiguousarray(inp['s5_w_glu'][0]), bglu=lay_vec(inp['s5_b_glu'][0]))
    if final:
        base.update(final_g=np.ascontiguousarray(inp['final_g'].reshape(1, -1)))
    maps = []
    for c in range(NCORE):
        m = dict(base)
        m['YT_in'] = np.ascontiguousarray(_halo_cols_T(cfg, segs, YT, c).reshape(16, 128, NTBH))
        m['x_in'] = _halo_rows(cfg, segs, x_all, c)
        mk_ = np.ones((128, 2 * NS), np.float32)
        if c == 0:
            mk_[:, 0::2] = 0
        if c == NCORE - 1:
            mk_[:, 1::2] = 0
        m['mask'] = mk_
        maps.append(m)
    res = run_bass_kernel_spmd(nc, maps, core_ids=list(range(NCORE)))
    out = np.zeros((NTOK, 2048), np.float32)
    for c in range(NCORE):
        X = res.results[c]["X_out"]
        r = 0
        for b in range(NS):
            sg = segs[b]
            out[cfg.off[b] + c * sg:cfg.off[b] + (c + 1) * sg] = X[r:r + sg]
            r += sg
    return out


def _run_C(cfg, inp, x1_all, cT, KB):
    NS, NTOK = cfg.NS, cfg.NTOK
    nc = bass.Bass("TRN2", target_bir_lowering=False)
    d = {}
    shapes = dict(ident=[128, 128], cT=[128, 16, NS], adaw_A=[2048, 4096], adab_A=[128, 32], norm1_g0=[128, 16], w_inA=[2048, 1280], x_all=[NTOK, 2048],
                  qkv_g=[128, 6], rot32=[32, 32], wuq=[128, 4, 96], wukv=[128, 2, 192], ropecs=[32, 2, max(cfg.seqs)], lqk=[128, 4, 64], subln_g=[128, 1], relb=[32, 1], ohrev=[32, 1536], bfar=[128, 2])
    for k, v in shapes.items():
        d[k] = _din(nc, k, v)
    d['PT'] = _dint(nc, "PT", [1280, NTOK])
    for n, r in (('QN', 64), ('QR', 32), ('KN', 64), ('KR', 32), ('DQ', 128), ('DK', 128)):
        d[n] = _dint(nc, n, [r, NTOK], BF16)
    d['VTK'] = _dint(nc, 'VTK', [NTOK, 128], BF16); d['DVT'] = _dint(nc, 'DVT', [NTOK, 128], BF16); d['VREV'] = _dint(nc, 'VREV', [1, 1536])
    d['YC'] = _dout(nc, 'YC', [128, NTOK]); d['YD'] = _dout(nc, 'YD', [128, NTOK])
    with contextlib.ExitStack() as es:
        pg = Prog(nc, es); pg.psum_pool(8)
        phase_A1(pg, cfg, d, ncc=10)
        phase_C2(pg, cfg, d)
        phase_C3(pg, cfg, d, 0.8 - 0.6 * _math.exp(-0.3), KB=KB)
        pg.finish()
    aw = np.ascontiguousarray(inp['ada_w'][1][:, :4096]); ab = lay_vec(inp['ada_b'][1][:4096]); g1 = lay_vec(inp['norm1_g'][1])
    base = dict(ident=np.eye(128, dtype=np.float32), cT=cT, adaw_A=aw, adab_A=ab, norm1_g0=g1, x_all=x1_all, rot32=rot32(), ropecs=ropecs(max(cfg.seqs)), ohrev=ohrev())
    W = inp['odd_w_in'][0]
    maps = []
    for h in range(NCORE):
        cc = colsC(h)
        wc = np.zeros((2048, 1280), np.float32)
        wc[:, 0:512] = W[:, cc[0]]; wc[:, 512:768] = W[:, cc[1]]; wc[:, 768:800] = W[:, cc[2]]
        wc[:, 896:1024] = W[:, cc[3]]; wc[:, 1024:1152] = W[:, cc[4]]; wc[:, 1152:1280] = W[:, cc[5]]
        m = dict(base); m['w_inA'] = wc
        m.update(attn_small(inp, h)); maps.append(m)
    res = run_bass_kernel_spmd(nc, maps, core_ids=list(range(NCORE)))
    YT = np.zeros((2048, NTOK), np.float32)
    for h in range(NCORE):
        YT[128 * h:128 * h + 128] = res.results[h]["YC"]
        YT[1024 + 128 * h:1024 + 128 * h + 128] = res.results[h]["YD"]
    return YT


def run_all(inp, seqs_prompt, seq_sample, TT=512, TS=2048, KB=2048):
    inp = {k: np.asarray(v) for k, v in inp.items()}
    nb = inp['x_prompt'].shape[0]
    cfg = Cfg([seqs_prompt] * nb + [seq_sample], TT=TT)
    NS = cfg.NS
    x_all = np.ascontiguousarray(np.concatenate([inp['x_prompt'].reshape(-1, 2048), inp['x_sample'].reshape(-1, 2048)], 0), np.float32)
    c_all = np.concatenate([inp['c_prompt'], inp['c_sample']], 0).astype(np.float32)
    cT = np.ascontiguousarray(c_all.reshape(NS, 16, 128).transpose(2, 1, 0))
    YT0 = _run_A(cfg, inp, x_all, cT, TS)
    x1 = _run_ffn(cfg, inp, 0, YT0, x_all, cT, glu=True, final=False)
    YT1 = _run_C(cfg, inp, x1, cT, KB)
    y = _run_ffn(cfg, inp, 1, YT1, x1, cT, glu=False, final=True)
    npr = nb * seqs_prompt
    return (y[:npr].reshape(nb, seqs_prompt, 2048), y[npr:].reshape(1, seq_sample, 2048))


def kernel(**inputs):
    return run_all(inputs, 8192, 16384)
```
